# Optimizing a Trainium2 kernel written in Bass

```python
import math
import jax, jax.numpy as jnp
from jax import lax
import numpy as np

D_MODEL = 2048
BATCH = 4
SEQ = 4096
DEPTH = 2

PLE_DIM = 256
D_FF = 5632
ALPHA = (2 * DEPTH) ** 0.25
BETA = (8 * DEPTH) ** -0.25
Q_BLOCK = 128
ROPE_THETA = 10000.0
NEG_INF = -1e30
LN_EPS = 1e-5
RMS_EPS = 1e-6

S5_WIDTH = 512
S5_GROUP = 16
S5_GROUPS = S5_WIDTH // S5_GROUP
S5_STATE = 64
S5_DT_MIN = 1e-3
S5_DT_MAX = 1e-1

MLA_HEADS = 4
MLA_Q_RANK = 512
MLA_KV_RANK = 128
MLA_NOPE = 128
MLA_ROPE = 64
MLA_V = 128

RET_HEADS = 4
RET_QK = 64
RET_V = 128
RET_CHUNK = 128

DIFF_HEADS = 4
DIFF_QK = 64
DIFF_V = 128

T5_BUCKETS = 32
T5_MAX_DIST = 128

IN_SPLIT_SIZES = (S5_WIDTH, MLA_Q_RANK, MLA_KV_RANK, MLA_ROPE,
                  RET_HEADS * RET_QK, RET_HEADS * RET_QK, RET_HEADS * RET_V, RET_HEADS * RET_V,
                  DIFF_HEADS * 2 * DIFF_QK, DIFF_HEADS * 2 * DIFF_QK, DIFF_HEADS * DIFF_V)
IN_WIDTH = sum(IN_SPLIT_SIZES)
MIX_WIDTH = S5_WIDTH + MLA_HEADS * MLA_V + RET_HEADS * RET_V + DIFF_HEADS * DIFF_V

kernel_name = 'hybrid_s5_mla_retention_diffattn_deepnorm'

F32 = jnp.float32


def layer_norm(x, g, b):
    xf = x.astype(F32)
    mu = jnp.mean(xf, -1, keepdims=True)
    var = jnp.mean(jnp.square(xf - mu), -1, keepdims=True)
    return ((xf - mu) * lax.rsqrt(var + LN_EPS) * g.astype(F32) + b.astype(F32)).astype(x.dtype)


def rms_norm(x, g):
    xf = x.astype(F32)
    xf = xf * lax.rsqrt(jnp.mean(jnp.square(xf), -1, keepdims=True) + RMS_EPS)
    return (xf * g.astype(F32)).astype(x.dtype)


def swiglu(x, w_gate, w_up, w_down):
    return (jax.nn.silu(x @ w_gate) * (x @ w_up)) @ w_down


def rope_cos_sin(positions, dim):
    inv_freq = 1.0 / (ROPE_THETA ** (jnp.arange(0, dim, 2, dtype=F32) / dim))
    ang = positions.astype(F32)[..., None] * inv_freq
    return jnp.cos(ang)[:, :, None, :], jnp.sin(ang)[:, :, None, :]


def apply_rope(x, cos, sin):
    x1, x2 = jnp.split(x.astype(F32), 2, axis=-1)
    return jnp.concatenate([x1 * cos - x2 * sin, x1 * sin + x2 * cos], -1).astype(x.dtype)


def t5_bucket(dist):
    n = jnp.maximum(dist, 0)
    max_exact = T5_BUCKETS // 2
    nf = jnp.maximum(n, 1).astype(F32)
    large = max_exact + (jnp.log(nf / max_exact) / math.log(T5_MAX_DIST / max_exact)
                         * (T5_BUCKETS - max_exact)).astype(jnp.int32)
    large = jnp.minimum(large, T5_BUCKETS - 1)
    return jnp.where(n < max_exact, n, large)


def causal_block_sweep(block_fn, n_pos):
    out = lax.map(block_fn, jnp.arange(n_pos // Q_BLOCK))
    out = jnp.moveaxis(out, 0, 1)
    return out.reshape(out.shape[0], n_pos, out.shape[3], out.shape[4])


def complex_linear_combine(e1, e2):
    a1r, a1i, b1r, b1i = e1
    a2r, a2i, b2r, b2i = e2
    return (a2r * a1r - a2i * a1i,
            a2r * a1i + a2i * a1r,
            a2r * b1r - a2i * b1i + b2r,
            a2r * b1i + a2i * b1r + b2i)


def s5_mixer(u, lam_re, lam_im, log_dt, b_re, b_im, c_re, c_im, d, w_glu, b_glu):
    bsz, n_pos, _ = u.shape
    uf = u.astype(F32).reshape(bsz, n_pos, S5_GROUPS, S5_GROUP)
    lr = lam_re.astype(F32)
    li = lam_im.astype(F32)
    dt = jnp.exp(log_dt.astype(F32))[:, None]
    mag = jnp.exp(lr * dt)
    ar = mag * jnp.cos(li * dt)
    ai = mag * jnp.sin(li * dt)
    den = lr * lr + li * li
    fr = ((ar - 1.0) * lr + ai * li) / den
    fi = (ai * lr - (ar - 1.0) * li) / den
    br = b_re.astype(F32)
    bi = b_im.astype(F32)
    bbr = fr[..., None] * br - fi[..., None] * bi
    bbi = fr[..., None] * bi + fi[..., None] * br
    bu_r = jnp.einsum('bsgc,gpc->bsgp', uf, bbr)
    bu_i = jnp.einsum('bsgc,gpc->bsgp', uf, bbi)
    a_r = jnp.broadcast_to(ar, bu_r.shape)
    a_i = jnp.broadcast_to(ai, bu_i.shape)
    _, _, h_r, h_i = lax.associative_scan(complex_linear_combine, (a_r, a_i, bu_r, bu_i), axis=1)
    y = (jnp.einsum('bsgp,gcp->bsgc', h_r, c_re.astype(F32))
         - jnp.einsum('bsgp,gcp->bsgc', h_i, c_im.astype(F32))
         + d.astype(F32).reshape(S5_GROUPS, S5_GROUP) * uf)
    y = jax.nn.gelu(y.reshape(bsz, n_pos, S5_WIDTH)).astype(u.dtype)
    return y * jax.nn.sigmoid(y @ w_glu + b_glu)


def mla_mixer(c_q, c_kv, k_r, cos, sin, q_norm_g, w_uq, kv_norm_g, w_ukv):
    bsz, n_pos, _ = c_q.shape
    q = (rms_norm(c_q, q_norm_g) @ w_uq).reshape(bsz, n_pos, MLA_HEADS, MLA_NOPE + MLA_ROPE)
    q = jnp.concatenate([q[..., :MLA_NOPE], apply_rope(q[..., MLA_NOPE:], cos, sin)], -1)
    kv = (rms_norm(c_kv, kv_norm_g) @ w_ukv).reshape(bsz, n_pos, MLA_HEADS, MLA_NOPE + MLA_V)
    k_rope = apply_rope(k_r[:, :, None, :], cos, sin)
    k = jnp.concatenate([kv[..., :MLA_NOPE],
                         jnp.broadcast_to(k_rope, (bsz, n_pos, MLA_HEADS, MLA_ROPE))], -1)
    v = kv[..., MLA_NOPE:]
    scale = (MLA_NOPE + MLA_ROPE) ** -0.5
    key_idx = jnp.arange(n_pos)

    def block(i):
        start = i * Q_BLOCK
        qb = lax.dynamic_slice_in_dim(q, start, Q_BLOCK, axis=1)
        s = jnp.einsum('bqhd,bkhd->bhqk', qb, k).astype(F32) * scale
        causal = (start + jnp.arange(Q_BLOCK))[:, None] >= key_idx[None, :]
        pr = jax.nn.softmax(jnp.where(causal, s, NEG_INF), axis=-1)
        return jnp.einsum('bhqk,bkhd->bqhd', pr.astype(v.dtype), v)

    return causal_block_sweep(block, n_pos).reshape(bsz, n_pos, MLA_HEADS * MLA_V)


def retention_mixer(q, k, v, g, cos, sin):
    bsz, n_pos, _ = q.shape
    n_chunks = n_pos // RET_CHUNK
    q = apply_rope(q.reshape(bsz, n_pos, RET_HEADS, RET_QK), cos, sin).astype(F32)
    k = apply_rope(k.reshape(bsz, n_pos, RET_HEADS, RET_QK), cos, sin).astype(F32) * (RET_QK ** -0.5)
    v = v.reshape(bsz, n_pos, RET_HEADS, RET_V).astype(F32)
    log_gamma = jnp.log(1.0 - jnp.power(2.0, -5.0 - jnp.arange(RET_HEADS, dtype=F32)))
    idx = jnp.arange(RET_CHUNK, dtype=F32)
    rel = idx[:, None] - idx[None, :]
    intra = jnp.where(rel >= 0, jnp.exp(log_gamma[:, None, None] * jnp.maximum(rel, 0.0)), 0.0)
    qc = q.reshape(bsz, n_chunks, RET_CHUNK, RET_HEADS, RET_QK)
    kc = k.reshape(bsz, n_chunks, RET_CHUNK, RET_HEADS, RET_QK)
    vc = v.reshape(bsz, n_chunks, RET_CHUNK, RET_HEADS, RET_V)
    scores = jnp.einsum('bnihd,bnjhd->bnhij', qc, kc) * intra
    inner = jnp.einsum('bnhij,bnjhv->bnihv', scores, vc)
    k_decay = jnp.exp(log_gamma[None, :] * (RET_CHUNK - 1 - idx)[:, None])
    kv = jnp.einsum('bnjhd,jh,bnjhv->bnhdv', kc, k_decay, vc)
    chunk_decay = jnp.exp(log_gamma * RET_CHUNK)[:, None, None]

    def step(state, kv_n):
        return state * chunk_decay + kv_n, state

    _, prev = lax.scan(step, jnp.zeros((bsz, RET_HEADS, RET_QK, RET_V), F32), jnp.moveaxis(kv, 1, 0))
    prev = jnp.moveaxis(prev, 0, 1)
    q_decay = jnp.exp(log_gamma[None, :] * (idx + 1.0)[:, None])
    cross = jnp.einsum('bnihd,ih,bnhdv->bnihv', qc, q_decay, prev)
    o = (inner + cross).reshape(bsz, n_pos, RET_HEADS, RET_V)
    mu = jnp.mean(o, -1, keepdims=True)
    var = jnp.mean(jnp.square(o - mu), -1, keepdims=True)
    o = (o - mu) * lax.rsqrt(var + LN_EPS)
    return (jax.nn.silu(g.astype(F32)) * o.reshape(bsz, n_pos, RET_HEADS * RET_V)).astype(g.dtype)


def diff_mixer(q, k, v, positions, rel_bias, lq1, lk1, lq2, lk2, subln_g, lambda_init):
    bsz, n_pos, _ = q.shape
    q = q.reshape(bsz, n_pos, DIFF_HEADS, 2, DIFF_QK)
    k = k.reshape(bsz, n_pos, DIFF_HEADS, 2, DIFF_QK)
    v = v.reshape(bsz, n_pos, DIFF_HEADS, DIFF_V)
    lam = (jnp.exp(jnp.sum(lq1.astype(F32) * lk1.astype(F32)))
           - jnp.exp(jnp.sum(lq2.astype(F32) * lk2.astype(F32))) + lambda_init)
    scale = DIFF_QK ** -0.5
    key_idx = jnp.arange(n_pos)
    table = rel_bias.astype(F32)

    def block(i):
        start = i * Q_BLOCK
        qb = lax.dynamic_slice_in_dim(q, start, Q_BLOCK, axis=1)
        pos_q = lax.dynamic_slice_in_dim(positions, start, Q_BLOCK, axis=1)
        bucket = t5_bucket(pos_q[:, :, None] - positions[:, None, :])
        bias = jnp.transpose(jnp.take(table, bucket, axis=0), (0, 3, 1, 2))
        s = jnp.einsum('bqhmd,bkhmd->bhmqk', qb, k).astype(F32) * scale + bias[:, :, None]
        causal = (start + jnp.arange(Q_BLOCK))[:, None] >= key_idx[None, :]
        pr = jax.nn.softmax(jnp.where(causal, s, NEG_INF), axis=-1)
        a = pr[:, :, 0] - lam * pr[:, :, 1]
        return jnp.einsum('bhqk,bkhd->bqhd', a.astype(v.dtype), v)

    o = causal_block_sweep(block, n_pos)
    o = rms_norm(o, subln_g) * (1.0 - lambda_init)
    return o.reshape(bsz, n_pos, DIFF_HEADS * DIFF_V)


def setup_inputs(seed: int = 0) -> dict:
    key = jax.random.key(seed)
    ks = list(jax.random.split(key, 64))
    ctr = [0]

    def nk():
        ctr[0] += 1
        return ks[ctr[0] - 1]

    def nrm(shape, scale):
        return scale * jax.random.normal(nk(), shape, F32)

    def gain(shape):
        return 1.0 + nrm(shape, 0.02)

    L, D, F = DEPTH, D_MODEL, D_FF
    G, P = S5_GROUPS, S5_STATE
    x = nrm((BATCH, SEQ, D), 1.0)
    p = nrm((L, BATCH, SEQ, PLE_DIM), 1.0)
    offsets = jax.random.randint(nk(), (BATCH, 1), 0, 1024, jnp.int32)
    positions = offsets + jnp.arange(SEQ, dtype=jnp.int32)[None, :]
    rel_bias = nrm((T5_BUCKETS, DIFF_HEADS), 0.5)
    ffn1_w_gate = nrm((L, D, F), D ** -0.5)
    ffn1_w_up = nrm((L, D, F), D ** -0.5)
    ffn1_w_down = nrm((L, F, D), BETA * F ** -0.5)
    ln1_g = gain((L, D))
    ln1_b = nrm((L, D), 0.02)
    w_in = nrm((L, D, IN_WIDTH), D ** -0.5)
    w_out = nrm((L, MIX_WIDTH, D), BETA * MIX_WIDTH ** -0.5)
    ln2_g = gain((L, D))
    ln2_b = nrm((L, D), 0.02)
    s5_lambda_re = -0.5 + nrm((L, G, P), 0.01)
    s5_lambda_im = jnp.pi * jnp.arange(P, dtype=F32)[None, None, :] + nrm((L, G, P), 0.01)
    s5_log_dt = jax.random.uniform(nk(), (L, G), F32, math.log(S5_DT_MIN), math.log(S5_DT_MAX))
    s5_b_re = nrm((L, G, P, S5_GROUP), (2.0 * S5_GROUP) ** -0.5)
    s5_b_im = nrm((L, G, P, S5_GROUP), (2.0 * S5_GROUP) ** -0.5)
    s5_c_re = nrm((L, G, S5_GROUP, P), (2.0 * P) ** -0.5)
    s5_c_im = nrm((L, G, S5_GROUP, P), (2.0 * P) ** -0.5)
    s5_d = nrm((L, S5_WIDTH), 1.0)
    s5_w_glu = nrm((L, S5_WIDTH, S5_WIDTH), S5_WIDTH ** -0.5)
    s5_b_glu = nrm((L, S5_WIDTH), 0.02)
    mla_q_norm_g = gain((L, MLA_Q_RANK))
    mla_w_uq = nrm((L, MLA_Q_RANK, MLA_HEADS * (MLA_NOPE + MLA_ROPE)), MLA_Q_RANK ** -0.5)
    mla_kv_norm_g = gain((L, MLA_KV_RANK))
    mla_w_ukv = nrm((L, MLA_KV_RANK, MLA_HEADS * (MLA_NOPE + MLA_V)), MLA_KV_RANK ** -0.5)
    diff_lambda_q1 = nrm((L, DIFF_QK), 0.1)
    diff_lambda_k1 = nrm((L, DIFF_QK), 0.1)
    diff_lambda_q2 = nrm((L, DIFF_QK), 0.1)
    diff_lambda_k2 = nrm((L, DIFF_QK), 0.1)
    diff_subln_g = gain((L, DIFF_V))
    ffn2_w_gate = nrm((L, D, F), D ** -0.5)
    ffn2_w_up = nrm((L, D, F), D ** -0.5)
    ffn2_w_down = nrm((L, F, D), BETA * F ** -0.5)
    ple_w_gate = nrm((L, D, D), D ** -0.5)
    ple_b_gate = nrm((L, D), 0.02)
    ple_w_proj = nrm((L, PLE_DIM, D), BETA * PLE_DIM ** -0.5)
    ln3_g = gain((L, D))
    ln3_b = nrm((L, D), 0.02)
    return {'x': x, 'p': p, 'positions': positions, 'rel_bias': rel_bias,
            'ffn1_w_gate': ffn1_w_gate, 'ffn1_w_up': ffn1_w_up, 'ffn1_w_down': ffn1_w_down,
            'ln1_g': ln1_g, 'ln1_b': ln1_b, 'w_in': w_in, 'w_out': w_out,
            'ln2_g': ln2_g, 'ln2_b': ln2_b,
            's5_lambda_re': s5_lambda_re, 's5_lambda_im': s5_lambda_im, 's5_log_dt': s5_log_dt,
            's5_b_re': s5_b_re, 's5_b_im': s5_b_im, 's5_c_re': s5_c_re, 's5_c_im': s5_c_im,
            's5_d': s5_d, 's5_w_glu': s5_w_glu, 's5_b_glu': s5_b_glu,
            'mla_q_norm_g': mla_q_norm_g, 'mla_w_uq': mla_w_uq,
            'mla_kv_norm_g': mla_kv_norm_g, 'mla_w_ukv': mla_w_ukv,
            'diff_lambda_q1': diff_lambda_q1, 'diff_lambda_k1': diff_lambda_k1,
            'diff_lambda_q2': diff_lambda_q2, 'diff_lambda_k2': diff_lambda_k2,
            'diff_subln_g': diff_subln_g,
            'ffn2_w_gate': ffn2_w_gate, 'ffn2_w_up': ffn2_w_up, 'ffn2_w_down': ffn2_w_down,
            'ple_w_gate': ple_w_gate, 'ple_b_gate': ple_b_gate, 'ple_w_proj': ple_w_proj,
            'ln3_g': ln3_g, 'ln3_b': ln3_b}


def reference(x, p, positions, rel_bias,
              ffn1_w_gate, ffn1_w_up, ffn1_w_down, ln1_g, ln1_b, w_in, w_out,
              ln2_g, ln2_b,
              s5_lambda_re, s5_lambda_im, s5_log_dt, s5_b_re, s5_b_im, s5_c_re, s5_c_im,
              s5_d, s5_w_glu, s5_b_glu,
              mla_q_norm_g, mla_w_uq, mla_kv_norm_g, mla_w_ukv,
              diff_lambda_q1, diff_lambda_k1, diff_lambda_q2, diff_lambda_k2, diff_subln_g,
              ffn2_w_gate, ffn2_w_up, ffn2_w_down, ple_w_gate, ple_b_gate, ple_w_proj,
              ln3_g, ln3_b):
    cos_m, sin_m = rope_cos_sin(positions, MLA_ROPE)
    cos_r, sin_r = rope_cos_sin(positions, RET_QK)
    split_points = [int(c) for c in np.cumsum(IN_SPLIT_SIZES)[:-1]]
    for i in range(DEPTH):
        h = 0.5 * swiglu(x, ffn1_w_gate[i], ffn1_w_up[i], ffn1_w_down[i])
        x = layer_norm(ALPHA * x + h, ln1_g[i], ln1_b[i])
        (s5_u, mla_cq, mla_ckv, mla_kr, ret_q, ret_k, ret_v, ret_g,
         diff_q, diff_k, diff_v) = jnp.split(x @ w_in[i], split_points, axis=-1)
        y_s5 = s5_mixer(s5_u, s5_lambda_re[i], s5_lambda_im[i], s5_log_dt[i], s5_b_re[i], s5_b_im[i],
                        s5_c_re[i], s5_c_im[i], s5_d[i], s5_w_glu[i], s5_b_glu[i])
        y_mla = mla_mixer(mla_cq, mla_ckv, mla_kr, cos_m, sin_m, mla_q_norm_g[i], mla_w_uq[i],
                          mla_kv_norm_g[i], mla_w_ukv[i])
        y_ret = retention_mixer(ret_q, ret_k, ret_v, ret_g, cos_r, sin_r)
        lambda_init = 0.8 - 0.6 * math.exp(-0.3 * i)
        y_diff = diff_mixer(diff_q, diff_k, diff_v, positions, rel_bias, diff_lambda_q1[i],
                            diff_lambda_k1[i], diff_lambda_q2[i], diff_lambda_k2[i],
                            diff_subln_g[i], lambda_init)
        mix = jnp.concatenate([y_s5, y_mla.astype(x.dtype), y_ret.astype(x.dtype),
                               y_diff.astype(x.dtype)], axis=-1) @ w_out[i]
        x = layer_norm(ALPHA * x + mix, ln2_g[i], ln2_b[i])
        gate = jax.nn.sigmoid(x @ ple_w_gate[i] + ple_b_gate[i])
        h = 0.5 * swiglu(x, ffn2_w_gate[i], ffn2_w_up[i], ffn2_w_down[i]) + gate * (p[i] @ ple_w_proj[i])
        x = layer_norm(ALPHA * x + h, ln3_g[i], ln3_b[i])
    return x
```

```python
import numpy as np
import concourse.bass as bass
import concourse.mybir as mybir

F32 = mybir.dt.float32
BF16 = mybir.dt.bfloat16
I32 = mybir.dt.int32
AF = mybir.ActivationFunctionType
ALU = mybir.AluOpType
AX = mybir.AxisListType

COMPUTE = ("tensor", "vector", "scalar", "gpsimd")
NSLOT = 6


class _Op:
    __slots__ = ("eng", "fn", "deps", "flag", "tok", "is_dma", "idx", "q", "_barriered")

    def __init__(self, eng, fn, is_dma=False):
        self.eng = eng
        self.fn = fn
        self.deps = []
        self.flag = False
        self.tok = None
        self.is_dma = is_dma


class Prog:
    def __init__(self, nc, same_engine_sync=True):
        self.nc = nc
        self.ops = {e: [] for e in COMPUTE + ("sync",)}
        self.state = {}
        self.same_engine_sync = same_engine_sync
        self.out_dmas = []

    def _st(self, k):
        s = self.state.get(k)
        if s is None:
            s = [None, []]
            self.state[k] = s
        return s

    def op(self, eng, fn, reads=(), writes=(), dma=False):
        o = _Op(eng, fn, is_dma=dma)
        deps = []
        for k in reads:
            s = self._st(k)
            if s[0] is not None:
                deps.append(s[0])
        for k in writes:
            s = self._st(k)
            if s[0] is not None:
                deps.append(s[0])
            deps.extend(s[1])
        seen = set()
        for d in deps:
            if id(d) in seen or d is o:
                continue
            seen.add(id(d))
            if (not d.is_dma) and d.eng == eng and not dma:
                if eng == "tensor" or not self.same_engine_sync:
                    continue
            o.deps.append(d)
            d.flag = True
        pend = getattr(self, "pending", None)
        if pend and pend.get(eng):
            for d in pend[eng]:
                if d is not o and id(d) not in seen and not ((not d.is_dma) and d.eng == eng and not dma):
                    o.deps.append(d)
                    seen.add(id(d))
            pend[eng] = []
        for k in reads:
            self._st(k)[1].append(o)
        for k in writes:
            s = self._st(k)
            s[0] = o
            s[1] = []
        self.ops[eng].append(o)
        return o

    def barrier(self):
        ops = []
        for e, lst in self.ops.items():
            last_c = None
            for o in lst:
                if o.is_dma:
                    if not getattr(o, "_barriered", False):
                        ops.append(o)
                        o._barriered = True
                else:
                    last_c = o
            if last_c is not None:
                ops.append(last_c)
                last_c.flag = True
        self.pending = {e: list(ops) for e in self.ops}

    def dma(self, out, in_, reads=(), writes=(), q="sync", is_output=False):
        o = self.op(q, lambda e, out=out, in_=in_: e.dma_start(out=out, in_=in_), reads, writes, dma=True)
        o.flag = True
        if is_output:
            self.out_dmas.append(o)
        return o

    def emit(self):
        nc = self.nc
        import contextlib
        stack = contextlib.ExitStack()
        sems = {}
        for e in COMPUTE:
            sems[e] = stack.enter_context(nc.semaphore("s_" + e))
        dsems = {}
        for q in self.ops:
            n_d = sum(1 for o in self.ops[q] if o.is_dma)
            if n_d:
                dsems[q] = [stack.enter_context(nc.semaphore("d_%s_%d" % (q, i))) for i in range(min(NSLOT, n_d))]
        for e, lst in self.ops.items():
            cnt = 0
            dcnt = 0
            for o in lst:
                if o.is_dma:
                    slot = dcnt % NSLOT
                    o.tok = (dsems[e][slot], 16 * (dcnt // NSLOT + 1))
                    o.idx = dcnt
                    dcnt += 1
                elif o.flag:
                    cnt += 1
                    o.tok = (sems[e], cnt)
        with nc.Block() as block:
            def make(e):
                lst = self.ops[e]

                def body(eng):
                    known = {}
                    dlist = [o for o in lst if o.is_dma]

                    def wait(tok):
                        s, v = tok
                        if known.get(id(s), 0) >= v:
                            return
                        known[id(s)] = v
                        eng.wait_ge(s, v)
                    for o in lst:
                        for d in sorted(o.deps, key=lambda d: -d.tok[1]):
                            wait(d.tok)
                        if o.is_dma:
                            if o.idx >= NSLOT:
                                wait(dlist[o.idx - NSLOT].tok)
                            o.fn(eng).then_inc(o.tok[0], 16)
                        else:
                            ins = o.fn(eng)
                            if o.flag:
                                ins.then_inc(o.tok[0], 1)
                    for o in dlist[-NSLOT:]:
                        wait(o.tok)
                return body
            for e in self.ops:
                if not self.ops[e]:
                    continue
                getattr(block, e)(make(e))
        stack.close()


import numpy as np
import math
import concourse.bass as bass
import concourse.mybir as mybir
from concourse.bass_utils import run_bass_kernel_spmd

D = 2048
FF = 5632
ALPHA = 4 ** 0.25
LN_EPS = 1e-5
TT = 512
IN_W = 4288


def new_nc():
    return bass.Bass("TRN2", target_bir_lowering=False)


class Ctx:
    def __init__(self, nc):
        import contextlib
        self.nc = nc
        self.st = contextlib.ExitStack()
        self.n = 0

    def sb(self, shape, dt, name=None):
        self.n += 1
        return self.st.enter_context(self.nc.sbuf_tensor("s_" + (name or ("sb%d" % self.n)), list(shape), dt))

    def ps(self, shape, dt=F32, name=None):
        self.n += 1
        return self.st.enter_context(self.nc.psum_tensor("p_" + (name or ("ps%d" % self.n)), list(shape), dt))

    def close(self):
        self.st.close()


def emit_transposes(P, C, src, src_key, dst, dst_key, ps, ident, ntt, ndc, evac_engs=("vector", "scalar")):
    k = 0
    for dc in range(ndc):
        b = ps[k % len(ps)]
        for tt in range(ntt):
            P.op("tensor", lambda e, b=b, tt=tt, dc=dc: e.transpose(out=b[0][:, tt * 128:(tt + 1) * 128], in_=src[:, tt, dc * 128:(dc + 1) * 128], identity=ident[:]),
                 reads=[src_key, "ident"], writes=[b[1]])
        eng = evac_engs[k % len(evac_engs)]
        if eng == "vector":
            P.op("vector", lambda e, b=b, dc=dc: e.tensor_copy(out=dst[:, dc, 0:ntt * 128], in_=b[0][:, 0:ntt * 128]), reads=[b[1]], writes=[dst_key])
        else:
            P.op("scalar", lambda e, b=b, dc=dc: e.activation(out=dst[:, dc, 0:ntt * 128], in_=b[0][:, 0:ntt * 128], func=AF.Copy), reads=[b[1]], writes=[dst_key])
        k += 1


def emit_ffn(P, C, xT, xT_key, hT, wg, wu, wd, ps, wq="gpsimd"):
    nc = P.nc
    wgv = wg.rearrange("(c p) f -> p c f", p=128)
    wuv = wu.rearrange("(c p) f -> p c f", p=128)
    FT = 256
    nld = FF // FT
    for l in range(nld):
        bi = l % 2
        gb, ub = C.wgu[bi]
        P.dma(gb[:], wgv[:, :, l * FT:(l + 1) * FT], writes=[(("wg", bi), j_) for j_ in range(8)], q=wq)
        P.dma(ub[:], wuv[:, :, l * FT:(l + 1) * FT], writes=[(("wu", bi), j_) for j_ in range(8)], q=wq)
        for s in range(FT // 128):
            ft = l * (FT // 128) + s
            pg = ps[(ft % 2) * 2]
            pu = ps[(ft % 2) * 2 + 1]
            for dc in range(16):
                P.op("tensor", lambda e, pg=pg, gb=gb, dc=dc, s=s: e.matmul(pg[0][:], lhsT=gb[:, dc, s * 128:(s + 1) * 128], rhs=xT[:, dc, :], start=(dc == 0), stop=(dc == 15)),
                     reads=[(("wg", bi), j_) for j_ in range(8)] + [(xT_key, dc)], writes=[pg[1]])
            for dc in range(16):
                P.op("tensor", lambda e, pu=pu, ub=ub, dc=dc, s=s: e.matmul(pu[0][:], lhsT=ub[:, dc, s * 128:(s + 1) * 128], rhs=xT[:, dc, :], start=(dc == 0), stop=(dc == 15)),
                     reads=[(("wu", bi), j_) for j_ in range(8)] + [(xT_key, dc)], writes=[pu[1]])
            sg = C.silu[ft % 2]
            P.op("scalar", lambda e, sg=sg, pg=pg: e.activation(out=sg[:], in_=pg[0][:], func=AF.Silu), reads=[pg[1]], writes=[("silu", ft % 2)])
            P.op("vector", lambda e, sg=sg, pu=pu, ft=ft: e.tensor_tensor(out=hT[:, ft, :], in0=pu[0][:], in1=sg[:], op=ALU.mult), reads=[pu[1], ("silu", ft % 2)], writes=[("hT", ft)])


def emit_down(P, C, hT, wd, ps4, evac, wq="gpsimd", nfc=44, ncols=2048, hkeys=None, lbl="wd"):
    FCG = 4 if nfc % 4 == 0 else 2
    wdv = wd.rearrange("(c p) f -> p c f", p=128)
    ng = nfc // FCG
    k = 0
    for n in range(ncols // 512):
        for gi in range(ng):
            bi = k % 2
            k += 1
            wb = C.wd[bi]
            P.dma(wb[:, 0:FCG, :], wdv[:, gi * FCG:(gi + 1) * FCG, n * 512:(n + 1) * 512], writes=[("wd", bi)], q=wq)
            for tt in range(4):
                for j in range(FCG):
                    fc = gi * FCG + j
                    hk = [("hT", fc)] if hkeys is None else hkeys(fc)
                    P.op("tensor", lambda e, tt=tt, j=j, fc=fc, wb=wb: e.matmul(ps4[tt][0][:], lhsT=hT[:, fc, tt * 128:(tt + 1) * 128], rhs=wb[:, j, :], start=(fc == 0), stop=(fc == nfc - 1)),
                         reads=[("wd", bi)] + hk, writes=[ps4[tt][1]])
        for tt in range(4):
            evac(tt, n, ps4[tt][0], ps4[tt][1])


def emit_ln(P, C, xt, xkey_fn, gam, bet, out, okey_fn, ntt=4):
    for tt in range(ntt):
        st = C.lnstat[tt % 2]
        skey = ("lnstat", tt % 2)
        for c in range(4):
            P.op("vector", lambda e, tt=tt, c=c, st=st: e.bn_stats(out=st[:, c * 6:(c + 1) * 6], in_=xt[:, tt, c * 512:(c + 1) * 512]), reads=[xkey_fn(tt)], writes=[skey])
        mv = C.lnmv[tt % 2]
        mkey = ("lnmv", tt % 2)
        P.op("vector", lambda e, st=st, mv=mv: e.bn_aggr(out=mv[:, 0:2], in_=st[:, 0:24]), reads=[skey], writes=[mkey])
        P.op("scalar", lambda e, mv=mv: e.activation(out=mv[:, 2:3], in_=mv[:, 1:2], func=AF.Sqrt, bias=C.eps_ln[:, 0:1], scale=1.0), reads=[mkey, "consts"], writes=[mkey])
        P.op("vector", lambda e, mv=mv: e.reciprocal(out=mv[:, 3:4], in_=mv[:, 2:3]), reads=[mkey], writes=[mkey])
        P.op("vector", lambda e, tt=tt, mv=mv: e.tensor_scalar(out=out[:, tt, :], in0=xt[:, tt, :], scalar1=mv[:, 0:1], scalar2=mv[:, 3:4], op0=ALU.subtract, op1=ALU.mult),
             reads=[xkey_fn(tt), mkey], writes=[okey_fn(tt)])
        P.op("gpsimd", lambda e, tt=tt: e.tensor_tensor(out=out[:, tt, :], in0=out[:, tt, :], in1=gam[:], op=ALU.mult), reads=[okey_fn(tt), "lng"], writes=[okey_fn(tt)])
        P.op("gpsimd", lambda e, tt=tt: e.tensor_tensor(out=out[:, tt, :], in0=out[:, tt, :], in1=bet[:], op=ALU.add), reads=[okey_fn(tt), "lnb"], writes=[okey_fn(tt)])


def alloc_common(nc, C):
    C.ps = [(C.ps_t([128, 512]), ("ps", i)) for i in range(8)] if False else None


def setup_rowlocal(nc, P, C, ident_d):
    C.psb = [(C.ps([128, 512], F32, "psb%d" % i), ("ps", i)) for i in range(8)]
    C.ident = C.sb([128, 128], F32, "ident_sb")
    P.dma(C.ident[:], ident_d[:, :], writes=["ident"])
    C.xt = C.sb([128, 4, D], F32, "xt")
    C.xT = C.sb([128, 16, TT], BF16, "xT")
    C.hT = C.sb([128, 44, TT], BF16, "hT")
    C.wgu = [(C.sb([128, 16, 256], BF16, "wg%d" % i), C.sb([128, 16, 256], BF16, "wu%d" % i)) for i in range(2)]
    C.wd = [C.sb([128, 4, 512], BF16, "wd%d" % i) for i in range(2)]
    C.silu = [C.sb([128, TT], BF16, "silu%d" % i) for i in range(2)]
    C.lnstat = [C.sb([128, 24], F32, "lnst%d" % i) for i in range(2)]
    C.lnmv = [C.sb([128, 4], F32, "lnmv%d" % i) for i in range(2)]
    C.gam = C.sb([128, D], F32, "gam")
    C.bet = C.sb([128, D], F32, "bet")
    C.eps_ln = C.sb([128, 1], F32, "epsln")
    P.op("vector", lambda e: e.memset(C.eps_ln[:], LN_EPS), writes=["consts"])


def build_pre(ntiles=4, do_proj=True):
    nc = new_nc()
    NT = ntiles * TT
    x = nc.dram_tensor("x", [NT, D], F32, kind="ExternalInput").ap()
    wg = nc.dram_tensor("wg", [D, FF], F32, kind="ExternalInput").ap()
    wu = nc.dram_tensor("wu", [D, FF], F32, kind="ExternalInput").ap()
    wd = nc.dram_tensor("wd", [FF, D], F32, kind="ExternalInput").ap()
    lng = nc.dram_tensor("lng", [1, D], F32, kind="ExternalInput").ap()
    lnb = nc.dram_tensor("lnb", [1, D], F32, kind="ExternalInput").ap()
    ident_d = nc.dram_tensor("ident", [128, 128], F32, kind="ExternalInput").ap()
    x1 = nc.dram_tensor("x1", [NT, D], F32, kind="ExternalOutput").ap()
    d = {}
    outs = {}
    if do_proj:
        d["w_in"] = nc.dram_tensor("w_in", [D, IN_W], F32, kind="ExternalInput").ap()
        d["pos"] = nc.dram_tensor("pos", [1, NT], I32, kind="ExternalInput").ap()
        d["cst"] = nc.dram_tensor("cst", [128, 8], F32, kind="ExternalInput").ap()
        d["qg"] = nc.dram_tensor("qg", [128, 4], F32, kind="ExternalInput").ap()
        d["kvg"] = nc.dram_tensor("kvg", [128, 1], F32, kind="ExternalInput").ap()
        d["w_uq"] = nc.dram_tensor("w_uq", [512, 768], F32, kind="ExternalInput").ap()
        d["w_ukv"] = nc.dram_tensor("w_ukv", [128, 1024], F32, kind="ExternalInput").ap()
        for nm, shp, dt in (("uT", [512, NT], BF16), ("mqT", [4, 192, NT], BF16), ("mkT", [4, 128, NT], BF16), ("mkrT", [64, NT], BF16),
                            ("mv", [NT, 512], BF16), ("rqT", [256, NT], BF16), ("rkT", [256, NT], BF16), ("rv", [NT, 512], BF16),
                            ("rg", [NT, 512], F32), ("dqT", [512, NT], BF16), ("dkT", [512, NT], BF16), ("dv", [NT, 512], BF16)):
            outs[nm] = nc.dram_tensor(nm, shp, dt, kind="ExternalOutput").ap()
    P = Prog(nc)
    C = Ctx(nc)
    setup_rowlocal(nc, P, C, ident_d)
    if do_proj:
        setup_proj(nc, P, C, d)
    P.dma(C.gam[:], lng.partition_broadcast(128), writes=["lng"])
    P.dma(C.bet[:], lnb.partition_broadcast(128), writes=["lnb"])
    xv = x.rearrange("(n t p) d -> n p t d", p=128, t=4)
    x1v = x1.rearrange("(n t p) d -> n p t d", p=128, t=4)
    for n in range(ntiles):
        P.dma(C.xt[:], xv[n], writes=[("xt", t) for t in range(4)])
        emit_xT(P, C)
        for tt in range(4):
            P.op("gpsimd", lambda e, tt=tt: e.tensor_scalar(out=C.xt[:, tt, :], in0=C.xt[:, tt, :], scalar1=float(ALPHA), scalar2=None, op0=ALU.mult), reads=[("xt", tt)], writes=[("xt", tt)])
        emit_ffn(P, C, C.xT, "xT", C.hT, wg, wu, wd, C.psb[0:4])

        def evac(tt, nb, ps_ap, ps_key):
            P.op("vector", lambda e: e.scalar_tensor_tensor(out=C.xt[:, tt, nb * 512:(nb + 1) * 512], in0=ps_ap[:], scalar=0.5, in1=C.xt[:, tt, nb * 512:(nb + 1) * 512], op0=ALU.mult, op1=ALU.add),
                 reads=[ps_key, ("xt", tt)], writes=[("xt", tt)])
        emit_down(P, C, C.hT, wd, C.psb[4:8], evac)
        emit_ln(P, C, C.xt, lambda tt: ("xt", tt), C.gam, C.bet, C.xt, lambda tt: ("xt", tt))
        P.dma(x1v[n], C.xt[:], reads=[("xt", t) for t in range(4)], is_output=True)
        if do_proj:
            emit_xT(P, C)
            emit_proj(P, C, d, n, outs)
    P.emit()
    C.close()
    return nc


def emit_xT(P, C):
    k = 0
    for dc in range(16):
        b = C.psb[k % 4]
        for tt in range(4):
            P.op("tensor", lambda e, b=b, tt=tt, dc=dc: e.transpose(out=b[0][:, tt * 128:(tt + 1) * 128], in_=C.xt[:, tt, dc * 128:(dc + 1) * 128], identity=C.ident[:]),
                 reads=[("xt", tt), "ident"], writes=[b[1]])
        if k % 2 == 0:
            P.op("vector", lambda e, b=b, dc=dc: e.tensor_copy(out=C.xT[:, dc, :], in_=b[0][:]), reads=[b[1]], writes=[("xT", dc)])
        else:
            P.op("scalar", lambda e, b=b, dc=dc: e.activation(out=C.xT[:, dc, :], in_=b[0][:], func=AF.Copy), reads=[b[1]], writes=[("xT", dc)])
        k += 1


TWO_PI = 2 * math.pi
CW1 = 6.28125
CW2 = TWO_PI - CW1


def emit_sincos(P, ang, akey, s_out, c_out, okey, tmp_i, tmp_a, tmp_b, tkey, np_, sign=None, eng="vector"):
    n_ = slice(0, np_)

    def wrap(dst, src, extra_shift):
        if extra_shift != 0.0:
            P.op(eng, lambda e: e.tensor_scalar(out=dst, in0=src, scalar1=float(extra_shift), scalar2=None, op0=ALU.add), reads=[tkey], writes=[tkey, okey])
            src2 = dst
        else:
            src2 = src
        P.op(eng, lambda e: e.tensor_scalar(out=tmp_b[n_], in0=src2, scalar1=float(math.pi), scalar2=float(-TWO_PI), op0=ALU.is_gt, op1=ALU.mult), reads=[tkey], writes=[tkey])
        P.op(eng, lambda e: e.tensor_tensor(out=dst, in0=src2, in1=tmp_b[n_], op=ALU.add), reads=[tkey], writes=[tkey, okey])
        P.op(eng, lambda e: e.tensor_scalar(out=tmp_b[n_], in0=dst, scalar1=float(-math.pi), scalar2=float(TWO_PI), op0=ALU.is_lt, op1=ALU.mult), reads=[tkey], writes=[tkey])
        P.op(eng, lambda e: e.tensor_tensor(out=dst, in0=dst, in1=tmp_b[n_], op=ALU.add), reads=[tkey], writes=[tkey, okey])
    P.op(eng, lambda e: e.tensor_scalar(out=tmp_i[n_], in0=ang, scalar1=float(1 / TWO_PI), scalar2=None, op0=ALU.mult), reads=[akey], writes=[tkey])
    P.op(eng, lambda e: e.tensor_copy(out=tmp_b[n_], in_=tmp_i[n_]), reads=[tkey], writes=[tkey])
    P.op(eng, lambda e: e.scalar_tensor_tensor(out=tmp_a[n_], in0=tmp_b[n_], scalar=-CW1, in1=ang, op0=ALU.mult, op1=ALU.add), reads=[tkey, akey], writes=[tkey])
    P.op(eng, lambda e: e.scalar_tensor_tensor(out=tmp_a[n_], in0=tmp_b[n_], scalar=-CW2, in1=tmp_a[n_], op0=ALU.mult, op1=ALU.add), reads=[tkey], writes=[tkey])
    wrap(s_out, tmp_a[n_], 0.0)
    wrap(c_out, tmp_a[n_], math.pi / 2)
    if sign is not None:
        P.op("scalar", lambda e: e.activation(out=s_out, in_=s_out, func=AF.Sin, scale=sign), reads=[tkey, "consts"], writes=[okey])
    else:
        P.op("scalar", lambda e: e.activation(out=s_out, in_=s_out, func=AF.Sin), reads=[tkey], writes=[okey])
    P.op("scalar", lambda e: e.activation(out=c_out, in_=c_out, func=AF.Sin), reads=[tkey], writes=[okey])


def setup_proj(nc, P, C, d):
    C.cst = C.sb([128, 8], F32, "cst")
    P.dma(C.cst[:], d["cst"][:, :], writes=["consts"])
    C.ones = C.sb([128, 128], F32, "ones")
    P.op("vector", lambda e: e.memset(C.ones[:], 1.0), writes=["ones"])
    C.eps_r = C.sb([128, 1], F32, "epsr")
    P.op("vector", lambda e: e.memset(C.eps_r[:], 1e-6), writes=["consts2"])
    C.qg = C.sb([128, 4], F32, "qg")
    C.kvg = C.sb([128, 1], F32, "kvg")
    P.dma(C.qg[:], d["qg"][:, :], writes=["qg"])
    P.dma(C.kvg[:], d["kvg"][:, :], writes=["kvg"])
    C.wuq = C.sb([128, 4, 768], BF16, "wuq")
    C.wuqs = C.sb([128, 4, 256], BF16, "wuqs")
    C.wukv = C.sb([128, 1024], BF16, "wukv")
    C.wv = C.sb([128, 512], BF16, "wv")
    wuqv = d["w_uq"].rearrange("(c p) f -> p c f", p=128)
    P.dma(C.wuq[:], wuqv, writes=["wuq"], q="gpsimd")
    for h in range(4):
        b0 = h * 192 + 128
        P.dma(C.wuqs[:, :, h * 64:h * 64 + 32], wuqv[:, :, b0 + 32:b0 + 64], writes=[("wuqs_p", 2 * h)], q="gpsimd")
        P.dma(C.wuqs[:, :, h * 64 + 32:h * 64 + 64], wuqv[:, :, b0:b0 + 32], writes=[("wuqs_p", 2 * h + 1)], q="gpsimd")
    P.dma(C.wukv[:], d["w_ukv"][:, :], writes=["wukv"], q="gpsimd")
    for h in range(4):
        P.dma(C.wv[:, h * 128:(h + 1) * 128], d["w_ukv"][:, h * 256 + 128:h * 256 + 256], writes=[("wv_p", h)], q="gpsimd")
    for fc in range(4):
        P.op("vector", lambda e, fc=fc: e.tensor_scalar(out=C.wuq[:, fc, :], in0=C.wuq[:, fc, :], scalar1=C.qg[:, fc:fc + 1], scalar2=None, op0=ALU.mult), reads=["wuq", "qg"], writes=["wuq"])
        P.op("vector", lambda e, fc=fc: e.tensor_scalar(out=C.wuqs[:, fc, :], in0=C.wuqs[:, fc, :], scalar1=C.qg[:, fc:fc + 1], scalar2=None, op0=ALU.mult), reads=[("wuqs_p", j_) for j_ in range(8)] + ["wuqs", "qg"], writes=["wuqs"])
    P.op("vector", lambda e: e.tensor_scalar(out=C.wukv[:], in0=C.wukv[:], scalar1=C.kvg[:, 0:1], scalar2=None, op0=ALU.mult), reads=["wukv", "kvg"], writes=["wukv"])
    P.op("vector", lambda e: e.tensor_scalar(out=C.wv[:], in0=C.wv[:], scalar1=C.kvg[:, 0:1], scalar2=None, op0=ALU.mult), reads=[("wv_p", j_) for j_ in range(4)] + ["kvg"], writes=["wv"])
    C.f = [C.sb([128, TT], F32, "f%d" % i) for i in range(8)]
    C.posi = C.sb([128, TT], I32, "posi")
    C.ki = C.sb([128, TT], I32, "ki")
    C.cosT = C.sb([128, TT], F32, "cosT")
    C.sinT = C.sb([128, TT], F32, "sinT")
    C.rq = C.sb([128, TT], F32, "rq")
    C.rkv = C.sb([128, TT], F32, "rkv")
    C.rcq = C.sb([64, TT], F32, "rcq")
    C.rsq = C.sb([64, TT], F32, "rsq")
    C.rkvt = C.sb([128, 8], F32, "rkvt")
    C.rgst = [C.sb([128, 512], F32, "rgst%d" % i) for i in range(2)]


def emit_proj(P, C, d, n, outs):
    t0 = n * TT
    w_in = d["w_in"].rearrange("(c p) f -> p c f", p=128)
    wbufs = [(C.wgu[0][0], ("wg", 0)), (C.wgu[0][1], ("wu", 0)), (C.wgu[1][0], ("wg", 1)), (C.wgu[1][1], ("wu", 1))]
    st = {"wb": 0, "ps": 0, "stage": 0}

    def next_wb():
        b = wbufs[st["wb"] % 4]
        st["wb"] += 1
        return b

    def next_ps():
        b = C.psb[st["ps"] % 8]
        st["ps"] += 1
        return b

    def next_stage():
        i = 8 + st["stage"] % 12
        st["stage"] += 1
        return C.hT[:, i, :], ("hT", i)

    def load_cols(pieces):
        wb, wk0 = next_wb()
        m = len(pieces)
        for i, (src0, ncol, dst0) in enumerate(pieces):
            ks = [(wk0, i)] if i < m - 1 else [(wk0, j) for j in range(i, 8)]
            P.dma(wb[:, :, dst0:dst0 + ncol], w_in[:, :, src0:src0 + ncol], writes=ks, q="gpsimd")
        return wb, wk0

    def fm(wb, wk, c0, M, pb):
        for dc in range(16):
            P.op("tensor", lambda e, dc=dc: e.matmul(pb[0][0:M, :], lhsT=wb[:, dc, c0:c0 + M], rhs=C.xT[:, dc, :], start=(dc == 0), stop=(dc == 15)), reads=[(wk, j_) for j_ in range(8)] + [("xT", dc)], writes=[pb[1]])

    P.dma(C.posi[:], d["pos"][:, t0:t0 + TT].partition_broadcast(128), writes=["posi"])
    P.op("vector", lambda e: e.tensor_copy(out=C.f[0][:], in_=C.posi[:]), reads=["posi"], writes=["f0"])
    P.op("vector", lambda e: e.tensor_scalar(out=C.f[0][:], in0=C.f[0][:], scalar1=C.cst[:, 0:1], scalar2=None, op0=ALU.mult), reads=["f0", "consts"], writes=["f0"])
    emit_sincos(P, C.f[0][:], "f0", C.sinT[:], C.cosT[:], "ropetab", C.ki, C.f[1], C.f[2], "sctmp", 128, sign=C.cst[:, 1:2])

    for j in range(2):
        wb, wk = load_cols([(j * 256, 256, 0)])
        for s in range(2):
            pb = next_ps()
            fm(wb, wk, s * 128, 128, pb)
            sg, sk = next_stage()
            P.op("scalar", lambda e, sg=sg, pb=pb: e.activation(out=sg, in_=pb[0][:], func=AF.Copy), reads=[pb[1]], writes=[sk])
            r0 = (j * 2 + s) * 128
            P.dma(outs["uT"][r0:r0 + 128, t0:t0 + TT], sg, reads=[sk], is_output=True)
    psq = next_ps()
    for j in range(2):
        wb, wk = load_cols([(512 + j * 256, 256, 0)])
        for s in range(2):
            fc = j * 2 + s
            pb = next_ps()
            fm(wb, wk, s * 128, 128, pb)
            P.op("scalar", lambda e, pb=pb, fc=fc: e.activation(out=C.hT[:, fc, :], in_=pb[0][:], func=AF.Copy), reads=[pb[1]], writes=[("hT", fc)])
            sq = C.f[3 + fc % 2]
            sqk = "f%d" % (3 + fc % 2)
            P.op("scalar", lambda e, pb=pb, sq=sq: e.activation(out=sq[:], in_=pb[0][:], func=AF.Square), reads=[pb[1]], writes=[sqk])
            P.op("tensor", lambda e, sq=sq, fc=fc: e.matmul(psq[0][:], lhsT=C.ones[:], rhs=sq[:], start=(fc == 0), stop=(fc == 3)), reads=[sqk, "ones"], writes=[psq[1]])
    P.op("scalar", lambda e: e.activation(out=C.rq[:], in_=psq[0][:], func=AF.Sqrt, bias=C.eps_r[:, 0:1], scale=1.0 / 512), reads=[psq[1], "consts2"], writes=["rq"])
    P.op("vector", lambda e: e.reciprocal(out=C.rq[:], in_=C.rq[:]), reads=["rq"], writes=["rq"])
    P.op("vector", lambda e: e.tensor_tensor(out=C.rcq[:], in0=C.rq[0:64, :], in1=C.cosT[0:64, :], op=ALU.mult), reads=["rq", "ropetab"], writes=["rcq"])
    P.op("vector", lambda e: e.tensor_tensor(out=C.rsq[:], in0=C.rq[0:64, :], in1=C.sinT[0:64, :], op=ALU.mult), reads=["rq", "ropetab"], writes=["rsq"])
    wb, wk = load_cols([(1024, 128, 0), (1152, 64, 128), (1152 + 32, 32, 192), (1152, 32, 224)])
    pb = next_ps()
    fm(wb, wk, 0, 128, pb)
    P.op("scalar", lambda e, pb=pb: e.activation(out=C.hT[:, 4, :], in_=pb[0][:], func=AF.Copy), reads=[pb[1]], writes=[("hT", 4)])
    P.op("scalar", lambda e, pb=pb: e.activation(out=C.f[5][:], in_=pb[0][:], func=AF.Square), reads=[pb[1]], writes=["f5"])
    pskv = next_ps()
    P.op("tensor", lambda e: e.matmul(pskv[0][:], lhsT=C.ones[:], rhs=C.f[5][:], start=True, stop=True), reads=["f5", "ones"], writes=[pskv[1]])
    P.op("scalar", lambda e: e.activation(out=C.rkv[:], in_=pskv[0][:], func=AF.Sqrt, bias=C.eps_r[:, 0:1], scale=1.0 / 128), reads=[pskv[1], "consts2"], writes=["rkv"])
    P.op("vector", lambda e: e.reciprocal(out=C.rkv[:], in_=C.rkv[:]), reads=["rkv"], writes=["rkv"])
    pst = next_ps()
    for tt in range(4):
        P.op("tensor", lambda e, tt=tt: e.matmul(pst[0][:, tt:tt + 1], lhsT=C.f[5][:, tt * 128:(tt + 1) * 128], rhs=C.ones[:, 0:1], start=True, stop=True), reads=["f5", "ones"], writes=[pst[1]])
    P.op("scalar", lambda e: e.activation(out=C.rkvt[:, 0:4], in_=pst[0][:, 0:4], func=AF.Sqrt, bias=C.eps_r[:, 0:1], scale=1.0 / 128), reads=[pst[1], "consts2"], writes=["rkvt"])
    P.op("vector", lambda e: e.reciprocal(out=C.rkvt[:, 0:4], in_=C.rkvt[:, 0:4]), reads=["rkvt"], writes=["rkvt"])
    pa = next_ps()
    pbb = next_ps()
    fm(wb, wk, 128, 64, pa)
    fm(wb, wk, 192, 64, pbb)
    P.op("vector", lambda e, pa=pa: e.tensor_tensor(out=C.f[6][0:64, :], in0=pa[0][0:64, :], in1=C.cosT[0:64, :], op=ALU.mult), reads=[pa[1], "ropetab"], writes=["f6"])
    P.op("vector", lambda e, pbb=pbb: e.tensor_tensor(out=C.f[7][0:64, :], in0=pbb[0][0:64, :], in1=C.sinT[0:64, :], op=ALU.mult), reads=[pbb[1], "ropetab"], writes=["f7"])
    sg, sk = next_stage()
    P.op("gpsimd", lambda e, sg=sg: e.tensor_tensor(out=sg[0:64, :], in0=C.f[6][0:64, :], in1=C.f[7][0:64, :], op=ALU.add), reads=["f6", "f7"], writes=[sk])
    P.dma(outs["mkrT"][:, t0:t0 + TT], sg[0:64, :], reads=[sk], is_output=True)
    for h in range(4):
        pb = next_ps()
        for fc in range(4):
            P.op("tensor", lambda e, fc=fc, h=h, pb=pb: e.matmul(pb[0][:], lhsT=C.wuq[:, fc, h * 192:h * 192 + 128], rhs=C.hT[:, fc, :], start=(fc == 0), stop=(fc == 3)), reads=["wuq", ("hT", fc)], writes=[pb[1]])
        sg, sk = next_stage()
        P.op("vector", lambda e, sg=sg, pb=pb: e.tensor_tensor(out=sg, in0=pb[0][:], in1=C.rq[:], op=ALU.mult), reads=[pb[1], "rq"], writes=[sk])
        P.dma(outs["mqT"][h, 0:128, t0:t0 + TT], sg, reads=[sk], is_output=True)
        pa = next_ps()
        pbb = next_ps()
        for fc in range(4):
            P.op("tensor", lambda e, fc=fc, h=h, pa=pa: e.matmul(pa[0][0:64, :], lhsT=C.wuq[:, fc, h * 192 + 128:h * 192 + 192], rhs=C.hT[:, fc, :], start=(fc == 0), stop=(fc == 3)), reads=["wuq", ("hT", fc)], writes=[pa[1]])
        for fc in range(4):
            P.op("tensor", lambda e, fc=fc, h=h, pbb=pbb: e.matmul(pbb[0][0:64, :], lhsT=C.wuqs[:, fc, h * 64:h * 64 + 64], rhs=C.hT[:, fc, :], start=(fc == 0), stop=(fc == 3)), reads=["wuqs", ("hT", fc)], writes=[pbb[1]])
        P.op("vector", lambda e, pa=pa: e.tensor_tensor(out=C.f[6][0:64, :], in0=pa[0][0:64, :], in1=C.rcq[:], op=ALU.mult), reads=[pa[1], "rcq"], writes=["f6"])
        P.op("vector", lambda e, pbb=pbb: e.tensor_tensor(out=C.f[7][0:64, :], in0=pbb[0][0:64, :], in1=C.rsq[:], op=ALU.mult), reads=[pbb[1], "rsq"], writes=["f7"])
        sg, sk = next_stage()
        P.op("gpsimd", lambda e, sg=sg: e.tensor_tensor(out=sg[0:64, :], in0=C.f[6][0:64, :], in1=C.f[7][0:64, :], op=ALU.add), reads=["f6", "f7"], writes=[sk])
        P.dma(outs["mqT"][h, 128:192, t0:t0 + TT], sg[0:64, :], reads=[sk], is_output=True)
        pb = next_ps()
        P.op("tensor", lambda e, h=h, pb=pb: e.matmul(pb[0][:], lhsT=C.wukv[:, h * 256:h * 256 + 128], rhs=C.hT[:, 4, :], start=True, stop=True), reads=["wukv", ("hT", 4)], writes=[pb[1]])
        sg, sk = next_stage()
        P.op("vector", lambda e, sg=sg, pb=pb: e.tensor_tensor(out=sg, in0=pb[0][:], in1=C.rkv[:], op=ALU.mult), reads=[pb[1], "rkv"], writes=[sk])
        P.dma(outs["mkT"][h, :, t0:t0 + TT], sg, reads=[sk], is_output=True)
    for tt in range(4):
        pb = next_ps()
        P.op("tensor", lambda e, tt=tt, pb=pb: e.matmul(pb[0][:], lhsT=C.hT[:, 4, tt * 128:(tt + 1) * 128], rhs=C.wv[:], start=True, stop=True), reads=["wv", ("hT", 4)], writes=[pb[1]])
        sg, sk = next_stage()
        P.op("vector", lambda e, sg=sg, pb=pb, tt=tt: e.tensor_scalar(out=sg, in0=pb[0][:], scalar1=C.rkvt[:, tt:tt + 1], scalar2=None, op0=ALU.mult), reads=[pb[1], "rkvt"], writes=[sk])
        P.dma(outs["mv"][t0 + tt * 128:t0 + (tt + 1) * 128, :], sg, reads=[sk], is_output=True)
    for (nm, c0) in (("rqT", 1216), ("rkT", 1472)):
        wa, wak = load_cols([(c0, 256, 0)])
        pieces = []
        for h in range(4):
            pieces.append((c0 + h * 64 + 32, 32, h * 64))
            pieces.append((c0 + h * 64, 32, h * 64 + 32))
        wsw, wswk = load_cols(pieces)
        for j in range(2):
            pa = next_ps()
            pbb = next_ps()
            fm(wa, wak, j * 128, 128, pa)
            fm(wsw, wswk, j * 128, 128, pbb)
            P.op("vector", lambda e, pa=pa: e.tensor_tensor(out=C.f[6][:], in0=pa[0][:], in1=C.cosT[:], op=ALU.mult), reads=[pa[1], "ropetab"], writes=["f6"])
            P.op("vector", lambda e, pbb=pbb: e.tensor_tensor(out=C.f[7][:], in0=pbb[0][:], in1=C.sinT[:], op=ALU.mult), reads=[pbb[1], "ropetab"], writes=["f7"])
            sg, sk = next_stage()
            P.op("gpsimd", lambda e, sg=sg: e.tensor_tensor(out=sg, in0=C.f[6][:], in1=C.f[7][:], op=ALU.add), reads=["f6", "f7"], writes=[sk])
            P.dma(outs[nm][j * 128:(j + 1) * 128, t0:t0 + TT], sg, reads=[sk], is_output=True)
    for (nm, c0) in (("dqT", 2752), ("dkT", 3264)):
        for j in range(2):
            wb, wk = load_cols([(c0 + j * 256, 256, 0)])
            for s in range(2):
                pb = next_ps()
                fm(wb, wk, s * 128, 128, pb)
                sg, sk = next_stage()
                P.op("scalar", lambda e, sg=sg, pb=pb: e.activation(out=sg, in_=pb[0][:], func=AF.Copy), reads=[pb[1]], writes=[sk])
                r0 = (j * 2 + s) * 128
                P.dma(outs[nm][r0:r0 + 128, t0:t0 + TT], sg, reads=[sk], is_output=True)
    k = 0
    for (nm, c0, isf32) in (("rv", 1728, False), ("rg", 2240, True), ("dv", 3776, False)):
        for j in range(2):
            wb, wk = load_cols([(c0 + j * 256, 256, 0)])
            for tt in range(4):
                pb = next_ps()
                for dc in range(16):
                    P.op("tensor", lambda e, dc=dc, tt=tt, pb=pb, wb=wb: e.matmul(pb[0][:, 0:256], lhsT=C.xT[:, dc, tt * 128:(tt + 1) * 128], rhs=wb[:, dc, 0:256], start=(dc == 0), stop=(dc == 15)), reads=[(wk, j_) for j_ in range(8)] + [("xT", dc)], writes=[pb[1]])
                if isf32:
                    sgt = C.rgst[k % 2]
                    sk = ("rgst", k % 2)
                    k += 1
                    sg = sgt[:, 0:256]
                else:
                    sgf, sk = next_stage()
                    sg = sgf[:, 0:256]
                P.op("scalar", lambda e, sg=sg, pb=pb: e.activation(out=sg, in_=pb[0][:, 0:256], func=AF.Copy), reads=[pb[1]], writes=[sk])
                P.dma(outs[nm][t0 + tt * 128:t0 + (tt + 1) * 128, j * 256:(j + 1) * 256], sg, reads=[sk], is_output=True)


def emit_attn(P, C, T, qch, kch, vaug, vkey, dvn, transform, post, tag):
    nqt = T // 512
    cnt = 0
    for qt in range(nqt):
        nkc = 4 * qt + 4
        for kc in range(nkc):
            i = kc - 4 * qt
            c0 = max(i, 0) * 128
            sb = C.psb[cnt % 3]
            pt = C.pt[cnt % 3]
            ptk = ("pt", cnt % 3)
            cnt += 1
            nch = len(qch)
            for ci in range(nch):
                qa, qk = qch[ci]
                ka, kk = kch[ci]
                P.op("tensor", lambda e, sb=sb, qa=qa, ka=ka, ci=ci, kc=kc, qt=qt, c0=c0: e.matmul(sb[0][:, c0:512], lhsT=ka[:, kc * 128:(kc + 1) * 128], rhs=qa[:, qt * 512 + c0:(qt + 1) * 512], start=(ci == 0), stop=(ci == nch - 1)),
                     reads=[qk, kk], writes=[sb[1]])
            transform(qt, kc, i, sb, (pt, ptk))
            for qs in range(4):
                if i > qs:
                    continue
                ob = C.psb[4 + qs]
                P.op("tensor", lambda e, ob=ob, pt=pt, qs=qs, kc=kc: e.matmul(ob[0][:, 0:dvn], lhsT=pt[:, qs * 128:(qs + 1) * 128], rhs=vaug[:, kc, 0:dvn], start=(kc == 0), stop=(kc == 4 * qt + qs)),
                     reads=[ptk] + list(vkey), writes=[ob[1]])
        for qs in range(4):
            post(qt, qs, C.psb[4 + qs])


def build_mix(T=4096, lam_init=0.2, parts=("mla", "diff", "ret", "s5")):
    nc = new_nc()
    NCH = T // 128
    dd = {}

    def din(nm, shp, dt=BF16):
        dd[nm] = nc.dram_tensor(nm, list(shp), dt, kind="ExternalInput").ap()
        return dd[nm]

    def dout(nm, shp, dt=F32):
        dd[nm] = nc.dram_tensor(nm, list(shp), dt, kind="ExternalOutput").ap()
        return dd[nm]
    din("mqT", [2, 192, T]); din("mkT", [2, 128, T]); din("mkrT", [64, T]); din("mv", [T, 256])
    din("rqT", [128, T]); din("rkT", [128, T]); din("rv", [T, 256]); din("rg", [T, 256], F32)
    din("dqT", [256, T]); din("dkT", [256, T]); din("dv", [T, 256])
    din("tri", [128, 128], F32)
    din("relb", [33, 2], F32)
    din("oh", [33, 1280], F32)
    din("dlam", [4, 64], F32)
    din("subg", [1, 128], F32)
    din("lng2", [1, 2], F32)
    din("iota_v", [128, 512], F32)
    din("gdl", [1, 40], F32)
    din("antiI", [128, 128], F32)
    dout("mla_o", [T, 256]); dout("ret_o", [T, 256]); dout("diff_o", [T, 256])
    din("uT", [256, T]); din("s5_lambda_re", [16, 64], F32); din("s5_lambda_im", [16, 64], F32); din("s5_log_dt", [1, 16], F32)
    din("s5_b_re", [16, 64, 16], F32); din("s5_b_im", [16, 64, 16], F32); din("s5_c_re", [16, 16, 64], F32); din("s5_c_im", [16, 16, 64], F32)
    din("s5_d", [1, 256], F32); din("ident", [128, 128], F32); din("tio", [128, 512], F32)
    dout("s5_yT", [256, T])
    bvec_d = nc.dram_tensor("bvec", [2, 1280], F32, kind="Internal").ap()
    P = Prog(nc)
    C = Ctx(nc)
    C.psb = [(C.ps([128, 512], F32, "psb%d" % i), ("ps", i)) for i in range(8)]
    C.pt = [C.sb([128, 512], BF16, "pt%d" % i) for i in range(3)]
    C.tri = C.sb([128, 128], BF16, "tri")
    trif = C.sb([128, 128], F32, "trif")
    P.dma(trif[:], dd["tri"][:, :], writes=["trif"])
    P.op("vector", lambda e: e.tensor_copy(out=C.tri[:], in_=trif[:]), reads=["trif"], writes=["tri"])
    vaug = C.sb([128, NCH, 132], BF16, "vaug")
    stage = [C.sb([128, 4, 128], F32, "stage%d" % i) for i in range(2)]
    rec = [C.sb([128, 8], F32, "rec%d" % i) for i in range(2)]
    eps6 = C.sb([128, 1], F32, "eps6"); eps5 = C.sb([128, 1], F32, "eps5")
    P.op("vector", lambda e: e.memset(eps6[:], 1e-6), writes=["eps6"])
    P.op("vector", lambda e: e.memset(eps5[:], 1e-5), writes=["eps5"])
    stc = {"n": 0}

    if "mla" in parts:
        scale_m = 192 ** -0.5
        Cp = Ctx(nc)
        qn = Cp.sb([128, T], BF16, "qn"); qr = Cp.sb([128, T], BF16, "qr")
        kn = Cp.sb([128, T], BF16, "kn"); kr = Cp.sb([128, T], BF16, "kr")
        P.dma(kr[0:64, :], dd["mkrT"][:, :], writes=["kr"])
        for h in range(2):
            P.dma(qn[:], dd["mqT"][h, 0:128, :], writes=["qn"])
            P.dma(qr[0:64, :], dd["mqT"][h, 128:192, :], writes=["qr"])
            P.dma(kn[:], dd["mkT"][h, :, :], writes=["kn"])
            P.dma(vaug[:, :, 0:128], dd["mv"].rearrange("(c p) f -> p c f", p=128)[:, :, h * 128:(h + 1) * 128], writes=["vaug"])
            P.op("gpsimd", lambda e: e.memset(vaug[:, :, 128:129], 1.0), writes=["vaug1"])

            def transform(qt, kc, i, sb, ptp):
                pt, ptk = ptp
                c0 = max(i, 0) * 128
                P.op("scalar", lambda e: e.activation(out=pt[:, c0:512], in_=sb[0][:, c0:512], func=AF.Exp, scale=float(scale_m)), reads=[sb[1]], writes=[ptk])
                if i >= 0:
                    P.op("gpsimd", lambda e: e.tensor_tensor(out=pt[:, c0:c0 + 128], in0=pt[:, c0:c0 + 128], in1=C.tri[:], op=ALU.mult), reads=[ptk, "tri"], writes=[ptk])

            def post(qt, qs, ob, h=h):
                k = stc["n"] % 2
                if qs == 0:
                    stc["cur"] = k
                k = stc["cur"]
                st_ = stage[k]
                rc = rec[k]
                P.op("vector", lambda e: e.reciprocal(out=rc[:, qs:qs + 1], in_=ob[0][:, 128:129]), reads=[ob[1]], writes=[("rec", k, qs)])
                P.op("vector", lambda e: e.tensor_scalar(out=st_[:, qs, :], in0=ob[0][:, 0:128], scalar1=rc[:, qs:qs + 1], scalar2=None, op0=ALU.mult), reads=[ob[1], ("rec", k, qs)], writes=[("stage", k, qs)])
                if qs == 3:
                    P.dma(dd["mla_o"][qt * 512:(qt + 1) * 512, h * 128:(h + 1) * 128].rearrange("(s p) f -> p s f", p=128), st_[:], reads=[("stage", k, j) for j in range(4)], is_output=True)
                    stc["n"] += 1
            emit_attn(P, C, T, [(qn[:], "qn"), (qr[0:64, :], "qr")], [(kn[:], "kn"), (kr[0:64, :], "kr")], vaug, ["vaug", "vaug1"], 129, transform, post, "mla%d" % h)
        P.barrier()
        Cp.close()
    if "diff" in parts:
        scale_d = 64 ** -0.5
        Cp = Ctx(nc)
        relb = Cp.sb([33, 2], F32, "relb")
        oh = Cp.sb([33, 1280], F32, "oh")
        bsb = Cp.sb([2, 1280], F32, "bsb")
        P.dma(relb[:], dd["relb"][:, :], writes=["relb"])
        P.dma(oh[:], dd["oh"][:, :], writes=["oh"])
        for j, (a, b) in enumerate(((0, 512), (512, 1024), (1024, 1280))):
            pb = C.psb[3]
            P.op("tensor", lambda e, a=a, b=b, pb=pb: e.matmul(pb[0][0:2, 0:b - a], lhsT=relb[:, :], rhs=oh[:, a:b], start=True, stop=True), reads=["relb", "oh"], writes=[pb[1]])
            P.op("vector", lambda e, a=a, b=b, pb=pb: e.tensor_copy(out=bsb[:, a:b], in_=pb[0][0:2, 0:b - a]), reads=[pb[1]], writes=[("bsb", j)])
        P.dma(bvec_d[:, :], bsb[:], reads=[("bsb", j) for j in range(3)], writes=["bvec_d"])
        cfar = Cp.sb([128, 2], F32, "cfar")
        P.dma(cfar[:], dd["relb"][31:32, :].partition_broadcast(128), writes=["cfar"])
        btile = Cp.sb([128, 2, 5, 512], F32, "btile")
        DELTAS = (128, 0, -128, -256, -384)
        antiI = Cp.sb([128, 128], F32, "antiI")
        P.dma(antiI[:], dd["antiI"][:, :], writes=["antiI"])
        hank = [Cp.sb([128, 512], F32, "hank%d" % i) for i in range(2)]
        hk = 0
        for h in range(2):
            for di, dl in enumerate(DELTAS):
                src = bass.AP(tensor=bvec_d.tensor, offset=h * 1280 + 385 + dl, ap=[[1, 128], [1, 512]])
                hb = hank[hk % 2]
                hkey = ("hank", hk % 2)
                hk += 1
                P.dma(hb[:], src, reads=["bvec_d"], writes=[hkey])
                pb = C.psb[3]
                P.op("tensor", lambda e, hb=hb, pb=pb: e.matmul(pb[0][:], lhsT=antiI[:], rhs=hb[:], start=True, stop=True), reads=["antiI", hkey], writes=[pb[1]])
                P.op("vector", lambda e, h=h, di=di, pb=pb: e.tensor_copy(out=btile[:, h, di, :], in_=pb[0][:]), reads=[pb[1]], writes=[("btile", h, di)])
        lt = Cp.sb([128, 4, 64], F32, "lt")
        for j in range(4):
            P.dma(lt[:, j, :], dd["dlam"][j:j + 1, :].partition_broadcast(128), writes=[("lt", j)])
        lsc = Cp.sb([128, 8], F32, "lsc")
        for j in range(2):
            P.op("vector", lambda e, j=j: e.tensor_tensor(out=lt[:, 2 * j, :], in0=lt[:, 2 * j, :], in1=lt[:, 2 * j + 1, :], op=ALU.mult), reads=[("lt", 2 * j), ("lt", 2 * j + 1)], writes=[("lt", 2 * j)])
            P.op("vector", lambda e, j=j: e.tensor_reduce(out=lsc[:, j:j + 1], in_=lt[:, 2 * j, :], axis=AX.X, op=ALU.add), reads=[("lt", 2 * j)], writes=[("lsc", j)])
            P.op("scalar", lambda e, j=j: e.activation(out=lsc[:, 2 + j:3 + j], in_=lsc[:, j:j + 1], func=AF.Exp), reads=[("lsc", j)], writes=[("lsc", 2 + j)])
        P.op("vector", lambda e: e.tensor_tensor(out=lsc[:, 4:5], in0=lsc[:, 3:4], in1=lsc[:, 2:3], op=ALU.subtract), reads=[("lsc", 2), ("lsc", 3)], writes=[("lsc", 4)])
        P.op("vector", lambda e: e.tensor_scalar(out=lsc[:, 5:6], in0=lsc[:, 4:5], scalar1=float(-lam_init), scalar2=None, op0=ALU.add), reads=[("lsc", 4)], writes=["neglam"])
        gsub = Cp.sb([128, 128], F32, "gsub")
        P.dma(gsub[:], dd["subg"].partition_broadcast(128), writes=["gsub"])
        P.op("vector", lambda e: e.tensor_scalar(out=gsub[:], in0=gsub[:], scalar1=float(1.0 - lam_init), scalar2=None, op0=ALU.mult), reads=["gsub"], writes=["gsub"])
        dq = Cp.sb([128, T], BF16, "dq"); dk = Cp.sb([128, T], BF16, "dk")
        o1 = Cp.sb([128, NCH, 128], F32, "o1")
        tmpb = [Cp.sb([128, 512], F32, "tmpb%d" % i) for i in range(2)]
        atmp = [Cp.sb([128, 128], F32, "atmp%d" % i) for i in range(2)]
        junk = Cp.sb([128, 128], F32, "junk")
        tbc = {"n": 0}
        for h in range(2):
            P.dma(dq[:], dd["dqT"][h * 128:(h + 1) * 128, :], writes=["dq"])
            P.dma(dk[:], dd["dkT"][h * 128:(h + 1) * 128, :], writes=["dk"])
            P.dma(vaug[:, :, 0:128], dd["dv"].rearrange("(c p) f -> p c f", p=128)[:, :, h * 128:(h + 1) * 128], writes=["vaug"])
            P.op("gpsimd", lambda e: e.memset(vaug[:, :, 128:129], 1.0), writes=["vaug1"])
            for m in range(2):
                def transform(qt, kc, i, sb, ptp, h=h):
                    pt, ptk = ptp
                    c0 = max(i, 0) * 128
                    dl = 128 * (4 * qt - kc)
                    if dl >= 256:
                        P.op("scalar", lambda e: e.activation(out=pt[:, c0:512], in_=sb[0][:, c0:512], func=AF.Exp, scale=float(scale_d), bias=cfar[:, h:h + 1]), reads=[sb[1], "cfar"], writes=[ptk])
                    else:
                        di = DELTAS.index(dl)
                        k = tbc["n"] % 2
                        tbc["n"] += 1
                        tb = tmpb[k]
                        P.op("vector", lambda e: e.scalar_tensor_tensor(out=tb[:, c0:512], in0=sb[0][:, c0:512], scalar=float(scale_d), in1=btile[:, h, di, c0:512], op0=ALU.mult, op1=ALU.add),
                             reads=[sb[1], ("btile", h, di)], writes=[("tmpb", k)])
                        P.op("scalar", lambda e: e.activation(out=pt[:, c0:512], in_=tb[:, c0:512], func=AF.Exp), reads=[("tmpb", k)], writes=[ptk])

                def post(qt, qs, ob, h=h, m=m):
                    ch = qt * 4 + qs
                    k = stc["n"] % 2
                    rc = rec[k]
                    st_ = stage[k]
                    P.op("vector", lambda e: e.reciprocal(out=rc[:, qs:qs + 1], in_=ob[0][:, 128:129]), reads=[ob[1]], writes=[("rec", k, qs)])
                    if m == 0:
                        P.op("vector", lambda e: e.tensor_scalar(out=o1[:, ch, :], in0=ob[0][:, 0:128], scalar1=rc[:, qs:qs + 1], scalar2=None, op0=ALU.mult), reads=[ob[1], ("rec", k, qs)], writes=[("o1", ch)])
                        if qs == 3:
                            stc["n"] += 1
                        return
                    at = atmp[qs % 2]
                    ak = ("atmp", qs % 2)
                    P.op("vector", lambda e: e.tensor_scalar(out=at[:], in0=ob[0][:, 0:128], scalar1=rc[:, qs:qs + 1], scalar2=None, op0=ALU.mult), reads=[ob[1], ("rec", k, qs)], writes=[ak])
                    P.op("vector", lambda e: e.scalar_tensor_tensor(out=at[:], in0=at[:], scalar=lsc[:, 5:6], in1=o1[:, ch, :], op0=ALU.mult, op1=ALU.add), reads=[ak, "neglam", ("o1", ch)], writes=[ak])
                    P.op("vector", lambda e: e.tensor_tensor(out=junk[:], in0=at[:], in1=at[:], op=ALU.mult), reads=[ak], writes=["junk"])
                    P.op("vector", lambda e: e.tensor_reduce(out=rc[:, 4 + qs:5 + qs], in_=junk[:], axis=AX.X, op=ALU.add), reads=["junk"], writes=[("rec", k, 4 + qs)])
                    P.op("scalar", lambda e: e.activation(out=rc[:, 4 + qs:5 + qs], in_=rc[:, 4 + qs:5 + qs], func=AF.Ln, scale=1.0 / 128, bias=eps6[:, 0:1]), reads=[("rec", k, 4 + qs), "eps6"], writes=[("rec", k, 4 + qs)])
                    P.op("scalar", lambda e: e.activation(out=rc[:, 4 + qs:5 + qs], in_=rc[:, 4 + qs:5 + qs], func=AF.Exp, scale=-0.5), reads=[("rec", k, 4 + qs)], writes=[("rec", k, 4 + qs)])
                    P.op("vector", lambda e: e.scalar_tensor_tensor(out=st_[:, qs, :], in0=at[:], scalar=rc[:, 4 + qs:5 + qs], in1=gsub[:], op0=ALU.mult, op1=ALU.mult), reads=[ak, ("rec", k, 4 + qs), "gsub"], writes=[("stage", k, qs)])
                    if qs == 3:
                        P.dma(dd["diff_o"][qt * 512:(qt + 1) * 512, h * 128:(h + 1) * 128].rearrange("(s p) f -> p s f", p=128), st_[:], reads=[("stage", k, j) for j in range(4)], is_output=True)
                        stc["n"] += 1
                emit_attn(P, C, T, [(dq[m * 64:(m + 1) * 64, :], "dq")], [(dk[m * 64:(m + 1) * 64, :], "dk")], vaug, ["vaug", "vaug1"], 129, transform, post, "diff")
        P.barrier()
        Cp.close()
    if "ret" in parts:
        Cp = Ctx(nc)
        lng = Cp.sb([128, 2], F32, "lng")
        P.dma(lng[:], dd["lng2"].partition_broadcast(128), writes=["lng"])
        iv = Cp.sb([128, 512], F32, "iv")
        P.dma(iv[:], dd["iota_v"][:, :], writes=["iv"])
        gd = Cp.sb([128, 40], F32, "gd")
        P.dma(gd[:], dd["gdl"].partition_broadcast(128), writes=["gd"])
        gsc = Cp.sb([128, 2, 40], F32, "gsc")
        rtab = Cp.sb([128, 2, 5, 512], F32, "rtab")
        vtmp = Cp.sb([128, 512], F32, "vtmp")
        for h in range(2):
            P.op("scalar", lambda e, h=h: e.activation(out=gsc[:, h, :], in_=gd[:], func=AF.Exp, scale=lng[:, h:h + 1]), reads=["gd", "lng"], writes=[("gsc", h)])
            P.op("vector", lambda e, h=h: e.tensor_scalar(out=gsc[:, h, :], in0=gsc[:, h, :], scalar1=0.125, scalar2=None, op0=ALU.mult), reads=[("gsc", h)], writes=[("gsc", h)])
            P.op("scalar", lambda e, h=h: e.activation(out=rtab[:, h, 4, :], in_=iv[:], func=AF.Exp, scale=lng[:, h:h + 1]), reads=["iv", "lng"], writes=[("rtab", h, 4)])
            for i in range(4):
                P.op("vector", lambda e, i=i: e.tensor_scalar(out=vtmp[:], in0=iv[:], scalar1=float(-128 * i), scalar2=None, op0=ALU.add), reads=["iv"], writes=["vtmp"])
                P.op("scalar", lambda e, h=h, i=i: e.activation(out=rtab[:, h, i, :], in_=vtmp[:], func=AF.Exp, scale=lng[:, h:h + 1]), reads=["vtmp", "lng"], writes=[("rtab", h, i)])
                P.op("vector", lambda e, i=i: e.tensor_scalar(out=vtmp[:], in0=vtmp[:], scalar1=0.0, scalar2=None, op0=ALU.is_ge), reads=["vtmp"], writes=["vtmp"])
                P.op("vector", lambda e, h=h, i=i: e.tensor_tensor(out=rtab[:, h, i, :], in0=rtab[:, h, i, :], in1=vtmp[:], op=ALU.mult), reads=["vtmp", ("rtab", h, i)], writes=[("rtab", h, i)])
        rqs = Cp.sb([128, T], BF16, "rqs"); rks = Cp.sb([128, T], BF16, "rks")
        P.dma(rqs[:], dd["rqT"][:, :], writes=["rqs"])
        P.dma(rks[:], dd["rkT"][:, :], writes=["rks"])
        gt_ = [Cp.sb([128, 4, 128], F32, "gt%d" % i) for i in range(2)]
        bst = [Cp.sb([128, 8], F32, "bst%d" % i) for i in range(2)]
        rt = [Cp.sb([128, 128], F32, "rt%d" % i) for i in range(2)]
        for h in range(2):
            P.dma(vaug[:, :, 0:128], dd["rv"].rearrange("(c p) f -> p c f", p=128)[:, :, h * 128:(h + 1) * 128], writes=["vaug"])

            def transform(qt, kc, i, sb, ptp, h=h):
                pt, ptk = ptp
                c0 = max(i, 0) * 128
                if i < 0:
                    j = -i
                    ti = 4
                else:
                    j = 0
                    ti = i
                P.op("vector", lambda e: e.scalar_tensor_tensor(out=pt[:, c0:512], in0=sb[0][:, c0:512], scalar=gsc[:, h, j:j + 1], in1=rtab[:, h, ti, c0:512], op0=ALU.mult, op1=ALU.mult),
                     reads=[sb[1], ("gsc", h), ("rtab", h, ti)], writes=[ptk])

            def post(qt, qs, ob, h=h):
                k = stc["n"] % 2
                st_ = stage[k]
                g_ = gt_[k]
                if qs == 0:
                    P.dma(g_[:], dd["rg"][qt * 512:(qt + 1) * 512, h * 128:(h + 1) * 128].rearrange("(s p) f -> p s f", p=128), writes=[("gt", k)])
                    P.op("scalar", lambda e: e.activation(out=st_[:], in_=g_[:], func=AF.Exp, scale=-1.0), reads=[("gt", k)], writes=[("stage", k, j) for j in range(4)])
                    P.op("vector", lambda e: e.tensor_scalar(out=st_[:], in0=st_[:], scalar1=1.0, scalar2=None, op0=ALU.add), reads=[("stage", k, 0)], writes=[("stage", k, j) for j in range(4)])
                    P.op("vector", lambda e: e.reciprocal(out=st_[:], in_=st_[:]), reads=[("stage", k, 0)], writes=[("stage", k, j) for j in range(4)])
                    P.op("vector", lambda e: e.tensor_tensor(out=g_[:], in0=g_[:], in1=st_[:], op=ALU.mult), reads=[("stage", k, 0), ("gt", k)], writes=[("gt", k)])
                bs = bst[qs % 2]
                bk = ("bst", qs % 2)
                P.op("vector", lambda e: e.bn_stats(out=bs[:, 0:6], in_=ob[0][:, 0:128]), reads=[ob[1]], writes=[bk])
                P.op("vector", lambda e: e.bn_aggr(out=bs[:, 6:8], in_=bs[:, 0:6]), reads=[bk], writes=[bk])
                P.op("scalar", lambda e: e.activation(out=bs[:, 7:8], in_=bs[:, 7:8], func=AF.Ln, bias=eps5[:, 0:1]), reads=[bk, "eps5"], writes=[bk])
                P.op("scalar", lambda e: e.activation(out=bs[:, 7:8], in_=bs[:, 7:8], func=AF.Exp, scale=-0.5), reads=[bk], writes=[bk])
                r_ = rt[qs % 2]
                P.op("vector", lambda e: e.tensor_scalar(out=r_[:], in0=ob[0][:, 0:128], scalar1=bs[:, 6:7], scalar2=bs[:, 7:8], op0=ALU.subtract, op1=ALU.mult), reads=[ob[1], bk], writes=[("rt", qs % 2)])
                P.op("vector", lambda e: e.tensor_tensor(out=st_[:, qs, :], in0=r_[:], in1=g_[:, qs, :], op=ALU.mult), reads=[("rt", qs % 2), ("gt", k)], writes=[("stage", k, qs)])
                if qs == 3:
                    P.dma(dd["ret_o"][qt * 512:(qt + 1) * 512, h * 128:(h + 1) * 128].rearrange("(s p) f -> p s f", p=128), st_[:], reads=[("stage", k, j) for j in range(4)], is_output=True)
                    stc["n"] += 1
            emit_attn(P, C, T, [(rqs[h * 64:(h + 1) * 64, :], "rqs")], [(rks[h * 64:(h + 1) * 64, :], "rks")], vaug, ["vaug"], 128, transform, post, "ret")
        P.barrier()
        Cp.close()
    if "s5" in parts:
        emit_s5(nc, P, C, dd, T)
    P.emit()
    C.close()
    return nc


def emit_s5(nc, P, C, dd, T):
    Cp = Ctx(nc)
    NT5 = T // 512
    sb = Cp.sb
    V = "vector"

    def tt(out, a, b, op, r, w, eng=V):
        P.op(eng, lambda e: e.tensor_tensor(out=out, in0=a, in1=b, op=op), reads=r, writes=w)

    def ts(out, a, s1, op0, r, w, s2=None, op1=None, eng=V):
        if op1 is None:
            P.op(eng, lambda e: e.tensor_scalar(out=out, in0=a, scalar1=s1, scalar2=None, op0=op0), reads=r, writes=w)
        else:
            P.op(eng, lambda e: e.tensor_scalar(out=out, in0=a, scalar1=s1, scalar2=s2, op0=op0, op1=op1), reads=r, writes=w)
    ident = sb([128, 128], F32, "s5ident")
    P.dma(ident[:], dd["ident"][:, :], writes=["s5ident"])
    tio = sb([128, 512], F32, "tio")
    P.dma(tio[:], dd["tio"][:, :], writes=["tio"])
    uT = sb([128, 2, T], BF16, "uT")
    for a in range(2):
        P.dma(uT[:, a, :], dd["uT"][a * 128:(a + 1) * 128, :], writes=[("uT", a)])
    rowmask = sb([128, 8], F32, "rowmask")
    P.op(V, lambda e: e.tensor_reduce(out=rowmask[:], in_=ident[:].rearrange("p (g c) -> p g c", c=16), axis=AX.X, op=ALU.add), reads=["s5ident"], writes=["rowmask"])
    W1 = sb([128, 16, 128], BF16, "W1"); W2 = sb([128, 16, 128], BF16, "W2")
    Dall = sb([128, 2, 128], BF16, "Dall")
    ki = sb([128, 512], I32, "s5ki"); ta = sb([128, 512], F32, "s5ta"); tb = sb([128, 512], F32, "s5tb")
    for a in range(2):
        LR = sb([128, 64], F32, "LR%d" % a); LI = sb([128, 64], F32, "LI%d" % a); LD = sb([128, 1], F32, "LD%d" % a)
        BR = sb([128, 64], F32, "BR%d" % a); BI = sb([128, 64], F32, "BI%d" % a)
        dcol = sb([128, 1], F32, "dcol%d" % a)
        for g8 in range(8):
            g = a * 8 + g8
            ps_ = slice(g8 * 16, (g8 + 1) * 16)
            P.dma(LR[ps_, :], dd["s5_lambda_re"][g:g + 1, :].partition_broadcast(16), writes=[("LR", a, g8)])
            P.dma(LI[ps_, :], dd["s5_lambda_im"][g:g + 1, :].partition_broadcast(16), writes=[("LI", a, g8)])
            P.dma(LD[ps_, :], dd["s5_log_dt"][0:1, g:g + 1].partition_broadcast(16), writes=[("LD", a, g8)])
            P.op("gpsimd", lambda e, g=g, ps_=ps_, BR=BR: e.dma_start(out=BR[ps_, :], in_=dd["s5_b_re"][g].rearrange("p c -> c p"), allow_slow_non_contiguous=True), writes=[("BR", a, g8)], dma=True)
            P.op("gpsimd", lambda e, g=g, ps_=ps_, BI=BI: e.dma_start(out=BI[ps_, :], in_=dd["s5_b_im"][g].rearrange("p c -> c p"), allow_slow_non_contiguous=True), writes=[("BI", a, g8)], dma=True)
        P.dma(dcol[:], dd["s5_d"][0:1, a * 128:(a + 1) * 128].rearrange("o (p f) -> (o p) f", f=1), writes=[("dcol", a)])
        ka = "A%d" % a
        allr = [("LR", a, j) for j in range(8)] + [("LI", a, j) for j in range(8)] + [("LD", a, j) for j in range(8)]
        dt = sb([128, 1], F32, "dt%d" % a)
        P.op("scalar", lambda e, dt=dt, LD=LD: e.activation(out=dt[:], in_=LD[:], func=AF.Exp), reads=allr, writes=[ka])
        mag = sb([128, 64], F32, "mag%d" % a); th = sb([128, 64], F32, "th%d" % a)
        ts(mag[:], LR[:], dt[:, 0:1], ALU.mult, allr + [ka], [ka + "mag"])
        P.op("scalar", lambda e, mag=mag: e.activation(out=mag[:], in_=mag[:], func=AF.Exp), reads=[ka + "mag"], writes=[ka + "mag"])
        ts(th[:], LI[:], dt[:, 0:1], ALU.mult, allr + [ka], [ka + "th"])
        sn = sb([128, 64], F32, "sn%d" % a); cs = sb([128, 64], F32, "cs%d" % a)
        emit_sincos(P, th[:], ka + "th", sn[:], cs[:], ka + "sc", ki[:, 0:64], ta[:, 0:64], tb[:, 0:64], "s5sctmp", 128)
        ar = cs; ai = sn
        tt(ar[:], cs[:], mag[:], ALU.mult, [ka + "sc", ka + "mag"], [ka + "ar"])
        tt(ai[:], sn[:], mag[:], ALU.mult, [ka + "sc", ka + "mag"], [ka + "ai"])
        ts(ar[:], ar[:], -1.0, ALU.add, [ka + "ar"], [ka + "ar"])
        den = sb([128, 64], F32, "den%d" % a); t2 = sb([128, 64], F32, "t2%d" % a)
        tt(den[:], LR[:], LR[:], ALU.mult, allr, [ka + "den"])
        tt(t2[:], LI[:], LI[:], ALU.mult, allr, [ka + "t2"])
        tt(den[:], den[:], t2[:], ALU.add, [ka + "den", ka + "t2"], [ka + "den"])
        P.op(V, lambda e, den=den: e.reciprocal(out=den[:], in_=den[:]), reads=[ka + "den"], writes=[ka + "den"])
        fr = sb([128, 64], F32, "fr%d" % a); fi = sb([128, 64], F32, "fi%d" % a)
        tt(fr[:], ar[:], LR[:], ALU.mult, [ka + "ar"] + allr, [ka + "fr"])
        tt(t2[:], ai[:], LI[:], ALU.mult, [ka + "ai", ka + "t2"] + allr, [ka + "t2"])
        tt(fr[:], fr[:], t2[:], ALU.add, [ka + "fr", ka + "t2"], [ka + "fr"])
        tt(fr[:], fr[:], den[:], ALU.mult, [ka + "fr", ka + "den"], [ka + "fr"])
        tt(fi[:], ai[:], LR[:], ALU.mult, [ka + "ai"] + allr, [ka + "fi"])
        tt(t2[:], ar[:], LI[:], ALU.mult, [ka + "ar", ka + "t2", ka + "fr"] + allr, [ka + "t2"])
        tt(fi[:], fi[:], t2[:], ALU.subtract, [ka + "fi", ka + "t2"], [ka + "fi"])
        tt(fi[:], fi[:], den[:], ALU.mult, [ka + "fi", ka + "den"], [ka + "fi"])
        brk = [("BR", a, j) for j in range(8)]; bik = [("BI", a, j) for j in range(8)]
        W1f = sb([128, 128], F32, "W1f%d" % a); W2f = sb([128, 128], F32, "W2f%d" % a)
        tt(W1f[:, 0:64], fr[:], BR[:], ALU.mult, [ka + "fr"] + brk, [ka + "w1a"])
        tt(t2[:], fi[:], BI[:], ALU.mult, [ka + "fi", ka + "t2"] + bik, [ka + "t2"])
        tt(W1f[:, 0:64], W1f[:, 0:64], t2[:], ALU.subtract, [ka + "w1a", ka + "t2"], [ka + "w1a"])
        tt(W1f[:, 64:128], fr[:], BI[:], ALU.mult, [ka + "fr"] + bik, [ka + "w1b"])
        tt(t2[:], fi[:], BR[:], ALU.mult, [ka + "fi", ka + "t2", ka + "w1a"] + brk, [ka + "t2"])
        tt(W1f[:, 64:128], W1f[:, 64:128], t2[:], ALU.add, [ka + "w1b", ka + "t2"], [ka + "w1b"])
        P.op(V, lambda e, W1f=W1f, W2f=W2f: e.tensor_copy(out=W2f[:, 0:64], in_=W1f[:, 64:128]), reads=[ka + "w1b"], writes=[ka + "w2a"])
        ts(W2f[:, 64:128], W1f[:, 0:64], -1.0, ALU.mult, [ka + "w1a"], [ka + "w2b"])
        for g8 in range(8):
            g = a * 8 + g8
            ts(W1[:, g, :], W1f[:], rowmask[:, g8:g8 + 1], ALU.mult, [ka + "w1a", ka + "w1b", "rowmask"], [("W1", g)])
            ts(W2[:, g, :], W2f[:], rowmask[:, g8:g8 + 1], ALU.mult, [ka + "w2a", ka + "w2b", "rowmask"], [("W2", g)])
        ts(Dall[:, a, :], ident[:], dcol[:, 0:1], ALU.mult, ["s5ident", ("dcol", a)], [("Dall", a)])
    LRB = sb([128, 16], F32, "LRB"); LIB = sb([128, 16], F32, "LIB"); LDB = sb([128, 16], F32, "LDB")
    for half in range(2):
        rs = slice(half * 64, (half + 1) * 64)
        P.op("gpsimd", lambda e, rs=rs: e.dma_start(out=LRB[rs, :], in_=dd["s5_lambda_re"].rearrange("g p -> p g"), allow_slow_non_contiguous=True), writes=[("LRB", half)], dma=True)
        P.op("gpsimd", lambda e, rs=rs: e.dma_start(out=LIB[rs, :], in_=dd["s5_lambda_im"].rearrange("g p -> p g"), allow_slow_non_contiguous=True), writes=[("LIB", half)], dma=True)
    P.dma(LDB[:], dd["s5_log_dt"].partition_broadcast(128), writes=["LDB"])
    P.op("scalar", lambda e: e.activation(out=LDB[:], in_=LDB[:], func=AF.Exp), reads=["LDB"], writes=["LDB"])
    magB = sb([128, 16], F32, "magB"); thB = sb([128, 16], F32, "thB")
    tt(magB[:], LRB[:], LDB[:], ALU.mult, [("LRB", 0), ("LRB", 1), "LDB"], ["magB"])
    P.op("scalar", lambda e: e.activation(out=magB[:], in_=magB[:], func=AF.Exp), reads=["magB"], writes=["magB"])
    tt(thB[:], LIB[:], LDB[:], ALU.mult, [("LIB", 0), ("LIB", 1), "LDB"], ["thB"])
    thn = sb([128, 16, 8], F32, "thn"); cn = sb([128, 16, 8], F32, "cn"); snn = sb([128, 16, 8], F32, "snn"); nsn = sb([128, 16, 8], F32, "nsn")
    for n in range(8):
        ts(thn[:, :, n], thB[:], float(512 * n), ALU.mult, ["thB"], [("thn", n)])
    emit_sincos(P, thn[:].rearrange("p g n -> p (g n)"), ("thn", 7), snn[:].rearrange("p g n -> p (g n)"), cn[:].rearrange("p g n -> p (g n)"), "cnsn", ki[:, 0:128], ta[:, 0:128], tb[:, 0:128], "s5sctmp", 128)
    ts(nsn[:].rearrange("p g n -> p (g n)"), snn[:].rearrange("p g n -> p (g n)"), -1.0, ALU.mult, ["cnsn"] + [("thn", n) for n in range(8)], ["nsn"])
    CT1f = sb([128, 16, 16], F32, "CT1f"); CT2f = sb([128, 16, 16], F32, "CT2f")
    CT1 = sb([128, 16, 16], BF16, "CT1"); CT2 = sb([128, 16, 16], BF16, "CT2")
    P.op("gpsimd", lambda e: e.dma_start(out=CT1f[0:64, 0:8], in_=dd["s5_c_re"][0:8].rearrange("g c p -> p g c"), allow_slow_non_contiguous=True), writes=[("CT1f", 0, 0)], dma=True)
    P.op("gpsimd", lambda e: e.dma_start(out=CT1f[0:64, 8:16], in_=dd["s5_c_re"][8:16].rearrange("g c p -> p g c"), allow_slow_non_contiguous=True), writes=[("CT1f", 0, 1)], dma=True)
    P.op("gpsimd", lambda e: e.dma_start(out=CT1f[64:128, 0:8], in_=dd["s5_c_im"][0:8].rearrange("g c p -> p g c"), allow_slow_non_contiguous=True), writes=[("CT1f", 1, 0)], dma=True)
    P.op("gpsimd", lambda e: e.dma_start(out=CT1f[64:128, 8:16], in_=dd["s5_c_im"][8:16].rearrange("g c p -> p g c"), allow_slow_non_contiguous=True), writes=[("CT1f", 1, 1)], dma=True)
    P.op("gpsimd", lambda e: e.dma_start(out=CT2f[0:64, 0:8], in_=dd["s5_c_im"][0:8].rearrange("g c p -> p g c"), allow_slow_non_contiguous=True), writes=[("CT2f", 0, 0)], dma=True)
    P.op("gpsimd", lambda e: e.dma_start(out=CT2f[0:64, 8:16], in_=dd["s5_c_im"][8:16].rearrange("g c p -> p g c"), allow_slow_non_contiguous=True), writes=[("CT2f", 0, 1)], dma=True)
    P.op("gpsimd", lambda e: e.dma_start(out=CT2f[64:128, 0:8], in_=dd["s5_c_re"][0:8].rearrange("g c p -> p g c"), allow_slow_non_contiguous=True), writes=[("CT2f", 1, 0)], dma=True)
    P.op("gpsimd", lambda e: e.dma_start(out=CT2f[64:128, 8:16], in_=dd["s5_c_re"][8:16].rearrange("g c p -> p g c"), allow_slow_non_contiguous=True), writes=[("CT2f", 1, 1)], dma=True)
    P.op(V, lambda e: e.tensor_copy(out=CT1[0:64], in_=CT1f[0:64]), reads=[("CT1f", 0, 0), ("CT1f", 0, 1)], writes=[("CT1", 0)])
    ts(CT1[64:128], CT1f[64:128], -1.0, ALU.mult, [("CT1f", 1, 0), ("CT1f", 1, 1)], [("CT1", 1)])
    ts(CT2[0:64], CT2f[0:64], -1.0, ALU.mult, [("CT2f", 0, 0), ("CT2f", 0, 1)], [("CT2", 0)])
    ts(CT2[64:128], CT2f[64:128], -1.0, ALU.mult, [("CT2f", 1, 0), ("CT2f", 1, 1)], [("CT2", 1)])
    ctk = [("CT1", 0), ("CT1", 1), ("CT2", 0), ("CT2", 1)]
    Cj = sb([128, 512], F32, "Cj"); Sj = sb([128, 512], F32, "Sj"); ang = sb([128, 512], F32, "s5ang")
    amat = sb([128, 512], F32, "amat"); ones5 = sb([128, 512], F32, "ones5")
    P.op(V, lambda e: e.memset(ones5[:], 1.0), writes=["ones5"])
    COSn = [sb([128, 512], F32, "COSn%d" % i) for i in range(2)]; SINn = [sb([128, 512], F32, "SINn%d" % i) for i in range(2)]
    Mt = [sb([128, 512], F32, "Mt%d" % i) for i in range(2)]; Gt = [sb([128, 512], F32, "Gt%d" % i) for i in range(2)]
    t1 = [sb([128, 512], F32, "t1_%d" % i) for i in range(2)]
    U1 = [sb([128, 512], BF16, "U1_%d" % i) for i in range(2)]; U2 = [sb([128, 512], BF16, "U2_%d" % i) for i in range(2)]
    ysb = [sb([16, T], F32, "ysb%d" % i) for i in range(2)]
    it = 0
    for g in range(16):
        a = g // 8
        g8 = g % 8
        ts(ang[:], tio[:], thB[:, g:g + 1], ALU.mult, ["tio", "thB"], ["s5ang"])
        emit_sincos(P, ang[:], "s5ang", Sj[:], Cj[:], "CjSj", ki, ta, tb, "s5sctmp", 128)
        ts(amat[:], ones5[:], magB[:, g:g + 1], ALU.mult, ["ones5", "magB"], ["amat"])
        yk = ("ysb", g % 2)
        for n in range(NT5):
            k = it % 2
            it += 1
            px1 = C.psb[2 * k]; px2 = C.psb[2 * k + 1]; py = C.psb[4 + k]
            P.op("tensor", lambda e, px1=px1, g=g, a=a, n=n: e.matmul(px1[0][:], lhsT=W1[:, g, :], rhs=uT[:, a, n * 512:(n + 1) * 512], start=True, stop=True), reads=[("W1", g), ("uT", a)], writes=[px1[1]])
            P.op("tensor", lambda e, px2=px2, g=g, a=a, n=n: e.matmul(px2[0][:], lhsT=W2[:, g, :], rhs=uT[:, a, n * 512:(n + 1) * 512], start=True, stop=True), reads=[("W2", g), ("uT", a)], writes=[px2[1]])
            cs_, sn_ = COSn[k], SINn[k]
            ck = ("cossin", k)
            ts(cs_[:], Cj[:], cn[:, g, n:n + 1], ALU.mult, ["CjSj", "cnsn"], [ck], eng="gpsimd")
            P.op(V, lambda e, cs_=cs_, g=g, n=n: e.scalar_tensor_tensor(out=cs_[:], in0=Sj[:], scalar=nsn[:, g, n:n + 1], in1=cs_[:], op0=ALU.mult, op1=ALU.add), reads=["CjSj", "nsn", ck], writes=[ck])
            ts(sn_[:], Sj[:], cn[:, g, n:n + 1], ALU.mult, ["CjSj", "cnsn"], [(ck, 1)], eng="gpsimd")
            P.op(V, lambda e, sn_=sn_, g=g, n=n: e.scalar_tensor_tensor(out=sn_[:], in0=Cj[:], scalar=snn[:, g, n:n + 1], in1=sn_[:], op0=ALU.mult, op1=ALU.add), reads=["CjSj", "cnsn", (ck, 1)], writes=[(ck, 1)])
            m_ = Mt[k]; t1_ = t1[k]; g_ = Gt[k]
            tt(t1_[:], px1[0][:], cs_[:], ALU.mult, [px1[1], ck], [("t1", k)])
            tt(m_[:], px2[0][:], sn_[:], ALU.mult, [px2[1], (ck, 1)], [("Mt", k)])
            tt(m_[:], m_[:], t1_[:], ALU.add, [("Mt", k), ("t1", k)], [("Mt", k)], eng="gpsimd")
            if n == 0:
                P.op(V, lambda e, g_=g_, m_=m_: e.tensor_tensor_scan(out=g_[:], data0=amat[:], data1=m_[:], initial=0.0, op0=ALU.mult, op1=ALU.add), reads=["amat", ("Mt", k)], writes=[("Gt", k)])
            else:
                gp = Gt[1 - k]
                P.op(V, lambda e, g_=g_, m_=m_, gp=gp: e.tensor_tensor_scan(out=g_[:], data0=amat[:], data1=m_[:], initial=gp[:, 511:512], op0=ALU.mult, op1=ALU.add), reads=["amat", ("Mt", k), ("Gt", 1 - k)], writes=[("Gt", k)])
            u1, u2 = U1[k], U2[k]
            tt(u1[:], cs_[:], g_[:], ALU.mult, [ck, ("Gt", k)], [("U1", k)], eng="gpsimd")
            tt(u2[:], sn_[:], g_[:], ALU.mult, [(ck, 1), ("Gt", k)], [("U2", k)], eng="gpsimd")
            P.op("tensor", lambda e, py=py, g=g, u1=u1: e.matmul(py[0][0:16, :], lhsT=CT1[:, g, :], rhs=u1[:], start=True, stop=False), reads=ctk + [("U1", k)], writes=[py[1]])
            P.op("tensor", lambda e, py=py, g=g, u2=u2: e.matmul(py[0][0:16, :], lhsT=CT2[:, g, :], rhs=u2[:], start=False, stop=False), reads=ctk + [("U2", k)], writes=[py[1]])
            P.op("tensor", lambda e, py=py, g8=g8, a=a, n=n: e.matmul(py[0][0:16, :], lhsT=Dall[:, a, g8 * 16:(g8 + 1) * 16], rhs=uT[:, a, n * 512:(n + 1) * 512], start=False, stop=True), reads=[("Dall", a), ("uT", a)], writes=[py[1]])
            P.op("scalar", lambda e, py=py, g=g, n=n: e.activation(out=ysb[g % 2][:, n * 512:(n + 1) * 512], in_=py[0][0:16, :], func=AF.Copy), reads=[py[1]], writes=[(yk, n)])
        P.dma(dd["s5_yT"][g * 16:(g + 1) * 16, :], ysb[g % 2][:], reads=[(yk, n) for n in range(NT5)], is_output=True)
    P.barrier()
    Cp.close()


def build_post(ntiles=4):
    nc = new_nc()
    NT = ntiles * TT
    dd = {}

    def din(nm, shp, dt=F32):
        dd[nm] = nc.dram_tensor(nm, list(shp), dt, kind="ExternalInput").ap()
        return dd[nm]
    din("x1", [NT, D]); din("s5y", [512, NT]); din("mo", [NT, 1536]); din("p", [NT, 256])
    din("w_glu", [512, 512]); din("b_glu", [128, 4]); din("w_out", [D, D])
    din("ln2g", [1, D]); din("ln2b", [1, D]); din("ln3g", [1, D]); din("ln3b", [1, D])
    din("wg", [D, FF]); din("wu", [D, FF]); din("wd", [FF, D])
    din("ple_wg", [D, D]); din("ple_bg", [1, D]); din("ple_wp", [256, D]); din("ident", [128, 128])
    x3 = nc.dram_tensor("x3", [NT, D], F32, kind="ExternalOutput").ap()
    P = Prog(nc)
    C = Ctx(nc)
    setup_rowlocal(nc, P, C, dd["ident"])
    sb = C.sb
    bpg = sb([128, D], F32, "bpg")
    P.dma(bpg[:], dd["ple_bg"].partition_broadcast(128), writes=["bpg"])
    wpp = sb([128, 2, D], BF16, "wpp")
    P.dma(wpp[:], dd["ple_wp"].rearrange("(c p) f -> p c f", p=128), writes=["wpp"], q="gpsimd")
    wglu = sb([128, 4, 512], BF16, "wglu")
    P.dma(wglu[:], dd["w_glu"].rearrange("(c p) f -> p c f", p=128), writes=["wglu"], q="gpsimd")
    bglu = sb([128, 4], F32, "bglu")
    P.dma(bglu[:], dd["b_glu"][:, :], writes=["bglu"])
    ys = sb([128, 4, TT], F32, "ys")
    ygb = sb([128, 4, TT], BF16, "ygb")
    sg = [sb([128, TT], F32, "sg%d" % i) for i in range(2)]
    mo = [sb([128, 1536], F32, "mo0")] * 2
    pt_ = sb([128, 4, 256], F32, "ptile")
    pT = sb([128, 2, TT], BF16, "pT")
    tmpa = [sb([128, 256], F32, "tmpa%d" % i) for i in range(2)]
    tmpc = [sb([128, 256], F32, "tmpc%d" % i) for i in range(2)]
    x1v = dd["x1"].rearrange("(n t p) d -> n p t d", p=128, t=4)
    x3v = x3.rearrange("(n t p) d -> n p t d", p=128, t=4)
    pv = dd["p"].rearrange("(n t p) d -> n p t d", p=128, t=4)
    s5v = dd["s5y"].rearrange("(c p) t -> p c t", p=128)
    V = "vector"
    wpg = dd["ple_wg"].rearrange("(c p) f -> p c f", p=128)
    for n in range(ntiles):
        t0 = n * TT
        P.dma(C.xt[:], x1v[n], writes=[("xt", t) for t in range(4)])
        for tt in range(4):
            P.op("gpsimd", lambda e, tt=tt: e.tensor_scalar(out=C.xt[:, tt, :], in0=C.xt[:, tt, :], scalar1=float(ALPHA), scalar2=None, op0=ALU.mult), reads=[("xt", tt)], writes=[("xt", tt)])
        P.dma(ys[:], s5v[:, :, t0:t0 + TT], writes=["ys"])
        for c in range(4):
            s_ = sg[c % 2]
            sk = ("sg", c % 2)
            P.op(V, lambda e, c=c, s_=s_: e.tensor_tensor(out=s_[:], in0=ys[:, c, :], in1=ys[:, c, :], op=ALU.mult), reads=["ys"], writes=[sk])
            P.op(V, lambda e, s_=s_: e.tensor_scalar(out=s_[:], in0=s_[:], scalar1=0.044715, scalar2=1.0, op0=ALU.mult, op1=ALU.add), reads=[sk], writes=[sk])
            P.op(V, lambda e, c=c, s_=s_: e.tensor_tensor(out=s_[:], in0=s_[:], in1=ys[:, c, :], op=ALU.mult), reads=[sk, "ys"], writes=[sk])
            P.op("scalar", lambda e, s_=s_: e.activation(out=s_[:], in_=s_[:], func=AF.Sigmoid, scale=1.5957691216057308), reads=[sk], writes=[sk])
            P.op(V, lambda e, c=c, s_=s_: e.tensor_tensor(out=ys[:, c, :], in0=ys[:, c, :], in1=s_[:], op=ALU.mult), reads=[sk, "ys"], writes=[("yg", c)])
            P.op("gpsimd", lambda e, c=c: e.tensor_copy(out=ygb[:, c, :], in_=ys[:, c, :]), reads=[("yg", c)], writes=[("ygb", c)])
        for j in range(4):
            pb = C.psb[j % 4]
            for c in range(4):
                P.op("tensor", lambda e, pb=pb, c=c, j=j: e.matmul(pb[0][:], lhsT=wglu[:, c, j * 128:(j + 1) * 128], rhs=ygb[:, c, :], start=(c == 0), stop=(c == 3)), reads=["wglu", ("ygb", c)], writes=[pb[1]])
            s_ = sg[j % 2]
            sk = ("sg", j % 2)
            P.op("scalar", lambda e, pb=pb, j=j, s_=s_: e.activation(out=s_[:], in_=pb[0][:], func=AF.Sigmoid, bias=bglu[:, j:j + 1]), reads=[pb[1], "bglu"], writes=[sk])
            P.op(V, lambda e, j=j, s_=s_: e.tensor_tensor(out=C.hT[:, j, :], in0=ys[:, j, :], in1=s_[:], op=ALU.mult), reads=[sk, ("yg", j)], writes=[("hT", j)])
        for tt in range(4):
            mb = mo[tt % 2]
            mk = ("mo", 0)
            P.dma(mb[:], dd["mo"][t0 + tt * 128:t0 + (tt + 1) * 128, :], writes=[mk])
            for fg in range(3):
                pb = C.psb[4 + (tt * 3 + fg) % 4]
                for j in range(4):
                    f = fg * 4 + j
                    P.op("tensor", lambda e, pb=pb, mb=mb, j=j, f=f: e.transpose(out=pb[0][:, j * 128:(j + 1) * 128], in_=mb[:, f * 128:(f + 1) * 128], identity=C.ident[:]), reads=[mk, "ident"], writes=[pb[1]])
                P.op("scalar" if fg % 2 else V, (lambda e, pb=pb, fg=fg, tt=tt: e.activation(out=C.hT[:, 4 + fg * 4:8 + fg * 4, tt * 128:(tt + 1) * 128], in_=pb[0][:].rearrange("p (j q) -> p j q", q=128), func=AF.Copy)) if fg % 2 else
                     (lambda e, pb=pb, fg=fg, tt=tt: e.tensor_copy(out=C.hT[:, 4 + fg * 4:8 + fg * 4, tt * 128:(tt + 1) * 128], in_=pb[0][:].rearrange("p (j q) -> p j q", q=128))),
                     reads=[pb[1]], writes=[("hTm", 4 + fg * 4 + j, tt) for j in range(4)] + [("hT", 4 + fg * 4 + j) for j in range(4)])

        def hkeys(fc):
            if fc < 4:
                return ("hT", fc)
            return ("hTmall", fc)

        def evac_wo(tt, nb, ps_ap, ps_key):
            P.op(V, lambda e: e.tensor_tensor(out=C.xt[:, tt, nb * 512:(nb + 1) * 512], in0=ps_ap[:], in1=C.xt[:, tt, nb * 512:(nb + 1) * 512], op=ALU.add), reads=[ps_key, ("xt", tt)], writes=[("xt", tt)])
        emit_down(P, C, C.hT, dd["w_out"], C.psb[0:4], evac_wo, nfc=16, hkeys=lambda fc: [("hT", fc)] if fc < 4 else [("hTm", fc, t_) for t_ in range(4)] + [("hT", fc)])
        P.dma(C.gam[:], dd["ln2g"].partition_broadcast(128), writes=["lng"])
        P.dma(C.bet[:], dd["ln2b"].partition_broadcast(128), writes=["lnb"])
        emit_ln(P, C, C.xt, lambda tt: ("xt", tt), C.gam, C.bet, C.xt, lambda tt: ("xt", tt))
        emit_xT(P, C)
        for tt in range(4):
            P.op("gpsimd", lambda e, tt=tt: e.tensor_scalar(out=C.xt[:, tt, :], in0=C.xt[:, tt, :], scalar1=float(ALPHA), scalar2=None, op0=ALU.mult), reads=[("xt", tt)], writes=[("xt", tt)])
        P.dma(pt_[:], pv[n], writes=["ptile"])
        pb = C.psb[0]
        for pc in range(2):
            pb = C.psb[pc]
            for tt in range(4):
                P.op("tensor", lambda e, pb=pb, tt=tt, pc=pc: e.transpose(out=pb[0][:, tt * 128:(tt + 1) * 128], in_=pt_[:, tt, pc * 128:(pc + 1) * 128], identity=C.ident[:]), reads=["ptile", "ident"], writes=[pb[1]])
            P.op(V, lambda e, pb=pb, pc=pc: e.tensor_copy(out=pT[:, pc, :], in_=pb[0][:]), reads=[pb[1]], writes=[("pT", pc)])
        wbufs = [(C.wgu[0][0], ("wg", 0)), (C.wgu[0][1], ("wu", 0)), (C.wgu[1][0], ("wg", 1)), (C.wgu[1][1], ("wu", 1))]
        kk = 0
        for hb in range(8):
            wb, wk = wbufs[hb % 4]
            c0 = hb * 256
            P.dma(wb[:], wpg[:, :, c0:c0 + 256], writes=[(wk, j_) for j_ in range(8)], q="gpsimd")
            for tt in range(4):
                pg = C.psb[(kk % 4) * 2]
                pp = C.psb[(kk % 4) * 2 + 1]
                ta_ = tmpa[kk % 2]
                tc_ = tmpc[kk % 2]
                tak = ("tmpa", kk % 2)
                tck = ("tmpc", kk % 2)
                kk += 1
                for dc in range(16):
                    P.op("tensor", lambda e, pg=pg, wb=wb, dc=dc, tt=tt: e.matmul(pg[0][:, 0:256], lhsT=C.xT[:, dc, tt * 128:(tt + 1) * 128], rhs=wb[:, dc, :], start=(dc == 0), stop=(dc == 15)),
                         reads=[(wk, j_) for j_ in range(8)] + [("xT", dc)], writes=[pg[1]])
                for pc in range(2):
                    P.op("tensor", lambda e, pp=pp, pc=pc, tt=tt, c0=c0: e.matmul(pp[0][:, 0:256], lhsT=pT[:, pc, tt * 128:(tt + 1) * 128], rhs=wpp[:, pc, c0:c0 + 256], start=(pc == 0), stop=(pc == 1)),
                         reads=["wpp", ("pT", pc)], writes=[pp[1]])
                P.op(V, lambda e, pg=pg, ta_=ta_, c0=c0: e.tensor_tensor(out=ta_[:], in0=pg[0][:, 0:256], in1=bpg[:, c0:c0 + 256], op=ALU.add), reads=[pg[1], "bpg"], writes=[tak])
                P.op("scalar", lambda e, ta_=ta_: e.activation(out=ta_[:], in_=ta_[:], func=AF.Sigmoid), reads=[tak], writes=[tak])
                P.op(V, lambda e, pp=pp, ta_=ta_, tc_=tc_: e.tensor_tensor(out=tc_[:], in0=pp[0][:, 0:256], in1=ta_[:], op=ALU.mult), reads=[pp[1], tak], writes=[tck])
                P.op("gpsimd", lambda e, tc_=tc_, tt=tt, c0=c0: e.tensor_tensor(out=C.xt[:, tt, c0:c0 + 256], in0=C.xt[:, tt, c0:c0 + 256], in1=tc_[:], op=ALU.add), reads=[tck, ("xt", tt)], writes=[("xt", tt)])
        emit_ffn(P, C, C.xT, "xT", C.hT, dd["wg"], dd["wu"], dd["wd"], C.psb[0:4])

        def evac2(tt, nb, ps_ap, ps_key):
            P.op(V, lambda e: e.scalar_tensor_tensor(out=C.xt[:, tt, nb * 512:(nb + 1) * 512], in0=ps_ap[:], scalar=0.5, in1=C.xt[:, tt, nb * 512:(nb + 1) * 512], op0=ALU.mult, op1=ALU.add),
                 reads=[ps_key, ("xt", tt)], writes=[("xt", tt)])
        emit_down(P, C, C.hT, dd["wd"], C.psb[4:8], evac2)
        P.dma(C.gam[:], dd["ln3g"].partition_broadcast(128), writes=["lng"])
        P.dma(C.bet[:], dd["ln3b"].partition_broadcast(128), writes=["lnb"])
        emit_ln(P, C, C.xt, lambda tt: ("xt", tt), C.gam, C.bet, C.xt, lambda tt: ("xt", tt))
        P.dma(x3v[n], C.xt[:], reads=[("xt", t) for t in range(4)], is_output=True)
    P.emit()
    C.close()
    return nc


_CACHE = {}


def _consts_pre():
    c = np.zeros((128, 8), np.float32)
    r = np.arange(128)
    invf = (1.0 / (10000.0 ** (np.arange(0, 64, 2, dtype=np.float32) / 64))).astype(np.float32)
    c[:, 0] = invf[r % 32]
    c[:, 1] = np.where((r % 64) < 32, -1.0, 1.0)
    return c


def _consts_mix(hh):
    c = {}
    c["tri"] = np.triu(np.ones((128, 128), np.float32))
    n = np.arange(1280) - 512
    nn = np.maximum(n, 0)
    nf = np.maximum(nn, 1).astype(np.float32)
    large = 16 + (np.log(nf / 16) / np.float32(math.log(128 / 16)) * 16).astype(np.int32)
    large = np.minimum(large, 31)
    bucket = np.where(nn < 16, nn, large)
    oh = np.zeros((33, 1280), np.float32)
    for m in range(1280):
        if n[m] < 0:
            oh[32, m] = 1
        else:
            oh[bucket[m], m] = 1
    c["oh"] = oh
    kl = np.arange(128)[:, None]
    ql = np.arange(512)[None, :]
    c["iota_v"] = (ql - kl).astype(np.float32)
    c["antiI"] = np.ascontiguousarray(np.eye(128, dtype=np.float32)[::-1])
    c["gdl"] = (128.0 * np.arange(40, dtype=np.float32))[None]
    lg = np.log(1.0 - np.power(2.0, -5.0 - np.arange(4, dtype=np.float32))).astype(np.float32)
    c["lng2"] = np.ascontiguousarray(lg[2 * hh:2 * hh + 2][None])
    c["ident"] = np.eye(128, dtype=np.float32)
    c["tio"] = np.ascontiguousarray(np.broadcast_to(np.arange(512, dtype=np.float32), (128, 512)))
    return c


def _get(name, fn):
    if name not in _CACHE:
        _CACHE[name] = fn()
    return _CACHE[name]


def _run(nc, maps):
    res = run_bass_kernel_spmd(nc, maps, core_ids=list(range(8)))
    return res.results


def kernel(**inp):
    A = np.ascontiguousarray
    x = np.asarray(inp["x"], np.float32)
    B_, S_, _ = x.shape
    HALF = S_ // 2
    ident = np.eye(128, dtype=np.float32)
    cpre = _consts_pre()
    xs = [A(x[c // 2, (c % 2) * HALF:(c % 2 + 1) * HALF]) for c in range(8)]
    pos = np.asarray(inp["positions"]).astype(np.int32)
    for L in range(2):
        g = lambda k: np.asarray(inp[k][L], np.float32)
        ncA = build_pre(4)
        shared = {"wg": A(g("ffn1_w_gate")), "wu": A(g("ffn1_w_up")), "wd": A(g("ffn1_w_down")), "lng": A(g("ln1_g")[None]), "lnb": A(g("ln1_b")[None]),
                  "ident": ident, "w_in": A(g("w_in")), "cst": cpre, "qg": A(g("mla_q_norm_g").reshape(4, 128).T), "kvg": A(g("mla_kv_norm_g").reshape(128, 1)),
                  "w_uq": A(g("mla_w_uq")), "w_ukv": A(g("mla_w_ukv"))}
        maps = []
        for c in range(8):
            m = dict(shared)
            m["x"] = xs[c]
            m["pos"] = A(pos[c // 2:c // 2 + 1, (c % 2) * HALF:(c % 2 + 1) * HALF])
            maps.append(m)
        ra = _run(ncA, maps)
        lam_init = 0.8 - 0.6 * math.exp(-0.3 * L)
        ncB = build_mix(T=S_, lam_init=lam_init)
        maps = []
        for c in range(8):
            b, hh = c // 2, c % 2
            r0, r1 = ra[2 * b], ra[2 * b + 1]
            cat1 = lambda k: np.concatenate([np.asarray(r0[k]), np.asarray(r1[k])], axis=-1)
            cat0 = lambda k: np.concatenate([np.asarray(r0[k]), np.asarray(r1[k])], axis=0)
            m = dict(_consts_mix(hh))
            m["uT"] = A(cat1("uT")[256 * hh:256 * hh + 256])
            m["mqT"] = A(cat1("mqT")[2 * hh:2 * hh + 2])
            m["mkT"] = A(cat1("mkT")[2 * hh:2 * hh + 2])
            m["mkrT"] = A(cat1("mkrT"))
            m["mv"] = A(cat0("mv")[:, 256 * hh:256 * hh + 256])
            m["rqT"] = A(cat1("rqT")[128 * hh:128 * hh + 128])
            m["rkT"] = A(cat1("rkT")[128 * hh:128 * hh + 128])
            m["rv"] = A(cat0("rv")[:, 256 * hh:256 * hh + 256])
            m["rg"] = A(cat0("rg")[:, 256 * hh:256 * hh + 256])
            m["dqT"] = A(cat1("dqT")[256 * hh:256 * hh + 256])
            m["dkT"] = A(cat1("dkT")[256 * hh:256 * hh + 256])
            m["dv"] = A(cat0("dv")[:, 256 * hh:256 * hh + 256])
            relb = np.zeros((33, 2), np.float32)
            relb[:32] = np.asarray(inp["rel_bias"], np.float32)[:, 2 * hh:2 * hh + 2]
            relb[32] = -30000.0
            m["relb"] = relb
            m["dlam"] = A(np.stack([g("diff_lambda_q1"), g("diff_lambda_k1"), g("diff_lambda_q2"), g("diff_lambda_k2")]))
            m["subg"] = A(g("diff_subln_g")[None])
            gs = slice(16 * hh, 16 * hh + 16)
            for k_ in ("s5_lambda_re", "s5_lambda_im", "s5_b_re", "s5_b_im", "s5_c_re", "s5_c_im"):
                m[k_] = A(g(k_)[gs])
            m["s5_log_dt"] = A(g("s5_log_dt")[gs][None])
            m["s5_d"] = A(g("s5_d")[256 * hh:256 * hh + 256][None])
            maps.append(m)
        rb = _run(ncB, maps)
        ncC = build_post(4)
        shared = {"w_glu": A(g("s5_w_glu")), "b_glu": A(g("s5_b_glu").reshape(4, 128).T), "w_out": A(g("w_out")),
                  "ln2g": A(g("ln2_g")[None]), "ln2b": A(g("ln2_b")[None]), "ln3g": A(g("ln3_g")[None]), "ln3b": A(g("ln3_b")[None]),
                  "wg": A(g("ffn2_w_gate")), "wu": A(g("ffn2_w_up")), "wd": A(g("ffn2_w_down")),
                  "ple_wg": A(g("ple_w_gate")), "ple_bg": A(g("ple_b_gate")[None]), "ple_wp": A(g("ple_w_proj")), "ident": ident}
        maps = []
        for c in range(8):
            b, half = c // 2, c % 2
            ts_ = slice(half * HALF, (half + 1) * HALF)
            b0, b1 = rb[2 * b], rb[2 * b + 1]
            m = dict(shared)
            m["x1"] = A(np.asarray(ra[c]["x1"]))
            m["s5y"] = A(np.concatenate([np.asarray(b0["s5_yT"])[:, ts_], np.asarray(b1["s5_yT"])[:, ts_]], axis=0))
            m["mo"] = A(np.concatenate([np.asarray(b0["mla_o"])[ts_], np.asarray(b1["mla_o"])[ts_], np.asarray(b0["ret_o"])[ts_], np.asarray(b1["ret_o"])[ts_],
                                        np.asarray(b0["diff_o"])[ts_], np.asarray(b1["diff_o"])[ts_]], axis=1))
            m["p"] = A(np.asarray(inp["p"], np.float32)[L, b, ts_])
            maps.append(m)
        rc = _run(ncC, maps)
        xs = [A(np.asarray(rc[c]["x3"])) for c in range(8)]
    out = np.zeros((B_, S_, x.shape[2]), np.float32)
    for c in range(8):
        out[c // 2, (c % 2) * HALF:(c % 2 + 1) * HALF] = xs[c]
    return out
```
